# Optimizing a Trainium2 kernel written in Bass

```python
import jax, jax.numpy as jnp
from jax import lax
import numpy as np

D_MODEL = 1024
BATCH = 8
SEQ = 2048
DEPTH = 4

EPS = 1e-6
F_FLOOR = 1e-30
GDN_HEADS = 4
GDN_HEAD_DIM = 128
GDN_WIDTH = GDN_HEADS * GDN_HEAD_DIM
GDN_CONV = 4
GDN_CHUNK = 64
HGRN_HEADS = 4
HGRN_STATE = 128
HGRN_HEAD_DIM = 128
HGRN_KEY_WIDTH = HGRN_HEADS * HGRN_STATE
HGRN_VAL_WIDTH = HGRN_HEADS * HGRN_HEAD_DIM
HGRN_CHUNK = 16
MIX_WIDTH = GDN_WIDTH + HGRN_VAL_WIDTH
AB_SIZES = (3 * GDN_WIDTH, GDN_WIDTH, GDN_HEADS, GDN_HEADS,
            HGRN_KEY_WIDTH, HGRN_KEY_WIDTH, HGRN_VAL_WIDTH, HGRN_VAL_WIDTH)
AB_COLS = sum(AB_SIZES)
LRU_WIDTH = D_MODEL
LRU_HEADS = 4
LRU_BLOCK = LRU_WIDTH // LRU_HEADS
LRU_CONV = 4
RG_C = 8.0
D_FF = 2816
FFN_CONV = 3
N_EVEN = (DEPTH + 1) // 2
N_ODD = DEPTH // 2

kernel_name = 'hybrid_gdn_hgrn2_rglru_convffn'


def _rmsnorm(x, gain):
    x32 = x.astype(jnp.float32)
    y = x32 * lax.rsqrt(jnp.mean(x32 * x32, axis=-1, keepdims=True) + EPS)
    return (y * gain.astype(jnp.float32)).astype(x.dtype)


def _l2norm(x):
    return x * lax.rsqrt(jnp.sum(x * x, axis=-1, keepdims=True) + EPS)


def _causal_dwconv(x, w):
    width, ch = w.shape
    return lax.conv_general_dilated(
        x, w[:, None, :].astype(x.dtype), window_strides=(1,), padding=[(width - 1, 0)],
        dimension_numbers=('NWC', 'WIO', 'NWC'), feature_group_count=ch)


def _heads(t, nh):
    b, s, _ = t.shape
    return t.reshape(b, s, nh, -1).transpose(0, 2, 1, 3)


def _masked_exp(logits, mask):
    return jnp.where(mask, jnp.exp(jnp.where(mask, logits, 0.0)), 0.0)


def _gated_delta_rule(q, k, v, g, beta):
    bsz, nh, seq, dk = q.shape
    dv = v.shape[-1]
    c = GDN_CHUNK
    n = seq // c
    q, k = (t.reshape(bsz, nh, n, c, dk) for t in (q, k))
    v = v.reshape(bsz, nh, n, c, dv)
    g, beta = (t.reshape(bsz, nh, n, c) for t in (g, beta))
    gc = jnp.cumsum(g, axis=-1)
    causal = jnp.tril(jnp.ones((c, c), dtype=bool))
    decay = _masked_exp(gc[..., :, None] - gc[..., None, :], causal)
    k_beta = k * beta[..., None]
    lower = jnp.tril(jnp.einsum('bhnik,bhnjk->bhnij', k_beta, k) * decay, -1)
    rhs = jnp.concatenate([v * beta[..., None], k_beta * jnp.exp(gc)[..., None]], axis=-1)
    sol = lax.linalg.triangular_solve(lower + jnp.eye(c, dtype=q.dtype), rhs, left_side=True,
                                      lower=True, unit_diagonal=True)
    u, w = sol[..., :dv], sol[..., dv:]
    attn = jnp.einsum('bhnik,bhnjk->bhnij', q, k) * decay
    q_dec = q * jnp.exp(gc)[..., None]
    k_dec = k * jnp.exp(gc[..., -1:] - gc)[..., None]
    chunk_decay = jnp.exp(gc[..., -1])

    def step(state, xs):
        u_c, w_c, attn_c, q_c, k_c, d_c = xs
        v_new = u_c - jnp.einsum('bhck,bhkv->bhcv', w_c, state)
        o_c = (jnp.einsum('bhck,bhkv->bhcv', q_c, state)
               + jnp.einsum('bhij,bhjv->bhiv', attn_c, v_new))
        state = state * d_c[..., None, None] + jnp.einsum('bhck,bhcv->bhkv', k_c, v_new)
        return state, o_c

    xs = tuple(jnp.moveaxis(t, 2, 0) for t in (u, w, attn, q_dec, k_dec, chunk_decay))
    _, o = lax.scan(step, jnp.zeros((bsz, nh, dk, dv), q.dtype), xs)
    return jnp.moveaxis(o, 0, 2).reshape(bsz, nh, seq, dv)


def _chunk_gla(q, k, v, log_f):
    bsz, nh, seq, dk = q.shape
    dv = v.shape[-1]
    c = HGRN_CHUNK
    n = seq // c
    q, k, log_f = (t.reshape(bsz, nh, n, c, dk) for t in (q, k, log_f))
    v = v.reshape(bsz, nh, n, c, dv)
    b = jnp.cumsum(log_f, axis=3)
    causal = jnp.tril(jnp.ones((c, c), dtype=bool))[:, :, None]
    rel = _masked_exp(b[..., :, None, :] - b[..., None, :, :], causal)
    scores = jnp.sum(q[..., :, None, :] * k[..., None, :, :] * rel, axis=-1)
    o_intra = jnp.einsum('bhnij,bhnjv->bhniv', scores, v)
    b_last = b[..., -1:, :]
    q_dec = q * jnp.exp(b)
    k_dec = k * jnp.exp(b_last - b)
    chunk_decay = jnp.exp(b_last[..., 0, :])

    def step(state, xs):
        q_c, k_c, v_c, d_c = xs
        o_c = jnp.einsum('bhck,bhkv->bhcv', q_c, state)
        state = state * d_c[..., None] + jnp.einsum('bhck,bhcv->bhkv', k_c, v_c)
        return state, o_c

    xs = tuple(jnp.moveaxis(t, 2, 0) for t in (q_dec, k_dec, v, chunk_decay))
    _, o_inter = lax.scan(step, jnp.zeros((bsz, nh, dk, dv), q.dtype), xs)
    return (o_intra + jnp.moveaxis(o_inter, 0, 2)).reshape(bsz, nh, seq, dv)


def _even_mixer(h, w_in, conv_w, a_log, dt_bias, gdn_gain, lower_bound, hgrn_gain, w_out):
    f32 = jnp.float32
    bsz, seq, _ = h.shape
    p = h @ w_in
    offsets = [int(o) for o in np.cumsum(AB_SIZES)[:-1]]
    qkv_a, z_a, beta_a, alpha_a, q_b, f_b, i_b, g_b = jnp.split(p, offsets, axis=-1)
    qkv = jax.nn.silu(_causal_dwconv(qkv_a, conv_w))
    q, k, v = (_heads(t, GDN_HEADS).astype(f32) for t in jnp.split(qkv, 3, axis=-1))
    q = _l2norm(q) * (GDN_HEAD_DIM ** -0.5)
    k = _l2norm(k)
    beta = jax.nn.sigmoid(beta_a.astype(f32)).transpose(0, 2, 1)
    g = (-jnp.exp(a_log.astype(f32))
         * jax.nn.softplus(alpha_a.astype(f32) + dt_bias.astype(f32))).transpose(0, 2, 1)
    o_a = _gated_delta_rule(q, k, v, g, beta).transpose(0, 2, 1, 3)
    z = jax.nn.silu(z_a.astype(f32)).reshape(bsz, seq, GDN_HEADS, GDN_HEAD_DIM)
    o_a = _rmsnorm(o_a, gdn_gain) * z
    f = lower_bound + (1.0 - lower_bound) * jax.nn.sigmoid(f_b.astype(f32))
    log_f = jnp.log(jnp.maximum(f, F_FLOOR))
    k_b = 1.0 - f
    q_b = jax.nn.silu(q_b.astype(f32))
    o_b = _chunk_gla(_heads(q_b, HGRN_HEADS), _heads(k_b, HGRN_HEADS),
                     _heads(i_b.astype(f32), HGRN_HEADS), _heads(log_f, HGRN_HEADS))
    o_b = o_b.transpose(0, 2, 1, 3)
    gate_b = jax.nn.silu(g_b.astype(f32)).reshape(bsz, seq, HGRN_HEADS, HGRN_HEAD_DIM)
    o_b = _rmsnorm(o_b, hgrn_gain) * gate_b
    o = jnp.concatenate([o_a.reshape(bsz, seq, GDN_WIDTH),
                         o_b.reshape(bsz, seq, HGRN_VAL_WIDTH)], axis=-1).astype(h.dtype)
    return o @ w_out


def _rglru_block(h, w_in, conv_w, conv_b, gate_a_w, gate_a_b, gate_x_w, gate_x_b, lam, w_out):
    f32 = jnp.float32
    bsz, seq, _ = h.shape
    y_branch, x_branch = jnp.split(h @ w_in, 2, axis=-1)
    gate = jax.nn.gelu(y_branch.astype(f32), approximate=True)
    xc = (_causal_dwconv(x_branch, conv_w) + conv_b).astype(f32)
    xb = xc.reshape(bsz, seq, LRU_HEADS, LRU_BLOCK)
    r = jax.nn.sigmoid(jnp.einsum('bthi,hij->bthj', xb, gate_a_w.astype(f32)).reshape(bsz, seq, LRU_WIDTH)
                       + gate_a_b.astype(f32))
    i = jax.nn.sigmoid(jnp.einsum('bthi,hij->bthj', xb, gate_x_w.astype(f32)).reshape(bsz, seq, LRU_WIDTH)
                       + gate_x_b.astype(f32))
    log_a = -RG_C * r * jax.nn.softplus(-lam.astype(f32))
    a = jnp.exp(log_a)
    u = jnp.sqrt(jnp.maximum(-jnp.expm1(2.0 * log_a), 0.0)) * (i * xc)

    def combine(left, right):
        a_l, b_l = left
        a_r, b_r = right
        return a_l * a_r, a_r * b_l + b_r

    _, hs = lax.associative_scan(combine, (a, u), axis=1)
    return (hs * gate).astype(h.dtype) @ w_out


def _conv_ffn(h, w_up, conv_w, conv_b, w_down):
    gate, val = jnp.split(h @ w_up, 2, axis=-1)
    gate = _causal_dwconv(gate, conv_w) + conv_b
    return (jax.nn.silu(gate) * val) @ w_down


def setup_inputs(seed: int = 0) -> dict:
    key = jax.random.key(seed)
    ks = iter(jax.random.split(key, 40))
    f32 = jnp.float32

    def nrm(shape, scale):
        return jax.random.normal(next(ks), shape, f32) * scale

    def uni(shape, lo, hi):
        return jax.random.uniform(next(ks), shape, f32, lo, hi)

    x = nrm((BATCH, SEQ, D_MODEL), 1.0)
    norm_mix = 1.0 + nrm((DEPTH, D_MODEL), 0.02)
    norm_ffn = 1.0 + nrm((DEPTH, D_MODEL), 0.02)
    norm_final = 1.0 + nrm((D_MODEL,), 0.02)
    ab_w_in = nrm((N_EVEN, D_MODEL, AB_COLS), D_MODEL ** -0.5)
    gdn_conv_w = nrm((N_EVEN, GDN_CONV, 3 * GDN_WIDTH), GDN_CONV ** -0.5)
    gdn_a_log = jnp.log(uni((N_EVEN, GDN_HEADS), 1.0, 16.0))
    dt = jnp.exp(uni((N_EVEN, GDN_HEADS), float(np.log(1e-3)), float(np.log(1e-1))))
    gdn_dt_bias = dt + jnp.log(-jnp.expm1(-dt))
    gdn_norm = 1.0 + nrm((N_EVEN, GDN_HEAD_DIM), 0.02)
    hgrn_lower_bounds = 1.0 + nrm((N_EVEN, HGRN_KEY_WIDTH), 0.1)
    hgrn_norm = 1.0 + nrm((N_EVEN, HGRN_HEAD_DIM), 0.02)
    ab_w_out = nrm((N_EVEN, MIX_WIDTH, D_MODEL), MIX_WIDTH ** -0.5)
    c_w_in = nrm((N_ODD, D_MODEL, 2 * LRU_WIDTH), D_MODEL ** -0.5)
    c_conv_w = nrm((N_ODD, LRU_CONV, LRU_WIDTH), LRU_CONV ** -0.5)
    c_conv_b = nrm((N_ODD, LRU_WIDTH), 0.01)
    c_gate_a_w = nrm((N_ODD, LRU_HEADS, LRU_BLOCK, LRU_BLOCK), LRU_BLOCK ** -0.5)
    c_gate_a_b = nrm((N_ODD, LRU_WIDTH), 0.01)
    c_gate_x_w = nrm((N_ODD, LRU_HEADS, LRU_BLOCK, LRU_BLOCK), LRU_BLOCK ** -0.5)
    c_gate_x_b = nrm((N_ODD, LRU_WIDTH), 0.01)
    a_c = uni((N_ODD, LRU_WIDTH), 0.9, 0.999) ** (1.0 / RG_C)
    c_lambda = jnp.log(a_c) - jnp.log1p(-a_c)
    c_w_out = nrm((N_ODD, LRU_WIDTH, D_MODEL), LRU_WIDTH ** -0.5)
    ffn_w_up = nrm((DEPTH, D_MODEL, 2 * D_FF), D_MODEL ** -0.5)
    ffn_conv_w = nrm((DEPTH, FFN_CONV, D_FF), FFN_CONV ** -0.5)
    ffn_conv_b = nrm((DEPTH, D_FF), 0.01)
    ffn_w_down = nrm((DEPTH, D_FF, D_MODEL), D_FF ** -0.5)
    return {'x': x, 'norm_mix': norm_mix, 'norm_ffn': norm_ffn, 'norm_final': norm_final,
            'ab_w_in': ab_w_in, 'gdn_conv_w': gdn_conv_w, 'gdn_a_log': gdn_a_log,
            'gdn_dt_bias': gdn_dt_bias, 'gdn_norm': gdn_norm, 'hgrn_lower_bounds': hgrn_lower_bounds,
            'hgrn_norm': hgrn_norm, 'ab_w_out': ab_w_out, 'c_w_in': c_w_in, 'c_conv_w': c_conv_w,
            'c_conv_b': c_conv_b, 'c_gate_a_w': c_gate_a_w, 'c_gate_a_b': c_gate_a_b,
            'c_gate_x_w': c_gate_x_w, 'c_gate_x_b': c_gate_x_b, 'c_lambda': c_lambda,
            'c_w_out': c_w_out, 'ffn_w_up': ffn_w_up, 'ffn_conv_w': ffn_conv_w,
            'ffn_conv_b': ffn_conv_b, 'ffn_w_down': ffn_w_down}


def reference(x, norm_mix, norm_ffn, norm_final, ab_w_in, gdn_conv_w, gdn_a_log, gdn_dt_bias,
              gdn_norm, hgrn_lower_bounds, hgrn_norm, ab_w_out, c_w_in, c_conv_w, c_conv_b,
              c_gate_a_w, c_gate_a_b, c_gate_x_w, c_gate_x_b, c_lambda, c_w_out,
              ffn_w_up, ffn_conv_w, ffn_conv_b, ffn_w_down):
    lb_p = jax.nn.softmax(hgrn_lower_bounds.astype(jnp.float32), axis=0)
    lower_bounds = jnp.cumsum(lb_p, axis=0) - lb_p[0]
    for layer in range(DEPTH):
        j = layer // 2
        h = _rmsnorm(x, norm_mix[layer])
        if layer % 2 == 0:
            mix = _even_mixer(h, ab_w_in[j], gdn_conv_w[j], gdn_a_log[j], gdn_dt_bias[j],
                              gdn_norm[j], lower_bounds[j], hgrn_norm[j], ab_w_out[j])
        else:
            mix = _rglru_block(h, c_w_in[j], c_conv_w[j], c_conv_b[j], c_gate_a_w[j],
                               c_gate_a_b[j], c_gate_x_w[j], c_gate_x_b[j], c_lambda[j], c_w_out[j])
        x = x + mix
        h = _rmsnorm(x, norm_ffn[layer])
        x = x + _conv_ffn(h, ffn_w_up[layer], ffn_conv_w[layer], ffn_conv_b[layer], ffn_w_down[layer])
    return _rmsnorm(x, norm_final)
```

```python
import contextlib
import numpy as np
import concourse.bass as bass
import concourse.mybir as mybir
from concourse.bass_utils import run_bass_kernel_spmd

F32 = mybir.dt.float32
BF16 = mybir.dt.bfloat16
AF = mybir.ActivationFunctionType
ALU = mybir.AluOpType

D = 1024
T = 2048
NCH = 8
DFF = 2816
NFC = 22
DEPTH = 4
EPS = 1e-6
ENGS = ("pe", "act", "dve", "pool", "sp")


class Buf:
    __slots__ = ("name", "w", "r", "excl")

    def __init__(self, name, excl=False):
        self.name = name
        self.w = None
        self.r = []
        self.excl = excl


class Instr:
    __slots__ = ("eng", "fn", "deps", "signal", "dma", "sem", "val")

    def __init__(self, eng, fn, dma):
        self.eng = eng
        self.fn = fn
        self.deps = []
        self.signal = False
        self.dma = dma
        self.sem = None
        self.val = None


class Prog:
    def __init__(self, nc, same_engine_sync=True):
        self.nc = nc
        self.streams = {e: [] for e in ENGS}
        self.same_engine_sync = same_engine_sync
        self.dma_slots = {}
        self.fence_deps = {e: [] for e in ENGS}
        self.count = 0

    def _add_dep(self, ins, d):
        if d is ins:
            return
        if d.dma is not None:
            d = self.dma_slots[d.dma][-1]
            if d is ins:
                return
        elif ins.dma is None and d.eng == ins.eng:
            if ins.eng == "pe" or not self.same_engine_sync:
                return
        ins.deps.append(d)
        d.signal = True

    def op(self, eng, fn, reads=(), writes=(), dma=None):
        ins = Instr(eng, fn, dma)
        self.count += 1
        deps = {}
        for b in reads:
            if b.w is not None:
                deps[id(b.w)] = b.w
            if b.excl:
                for r in b.r:
                    if r.eng != eng:
                        deps[id(r)] = r
        for b in writes:
            if b.w is not None:
                deps[id(b.w)] = b.w
            for r in b.r:
                deps[id(r)] = r
        for d in self.fence_deps[eng]:
            deps[id(d)] = d
        self.fence_deps[eng] = []
        for d in deps.values():
            self._add_dep(ins, d)
        key = eng if dma is None else ("dma", dma)
        for b in reads:
            b.r = [r for r in b.r if (r.eng if r.dma is None else ("dma", r.dma)) != key]
            b.r.append(ins)
        for b in writes:
            b.w = ins
            b.r = []
        self.streams[eng].append(ins)
        if dma is not None:
            self.dma_slots.setdefault(dma, []).append(ins)
        return ins

    def fence(self):
        last = []
        for e in ENGS:
            for ins in reversed(self.streams[e]):
                if ins.dma is None:
                    last.append(ins)
                    break
        for lst in self.dma_slots.values():
            last.append(lst[-1])
        for e in ENGS:
            self.fence_deps[e] = list(last)

    def emit(self, final_waits=()):
        nc = self.nc
        for f in final_waits:
            f.signal = True
        with contextlib.ExitStack() as es:
            eng_sem = {e: es.enter_context(nc.semaphore("s_" + e)) for e in ENGS}
            slot_sem = {s: es.enter_context(nc.semaphore("d_" + str(s))) for s in self.dma_slots}
            for e in ENGS:
                c = 0
                for ins in self.streams[e]:
                    if ins.dma is None and ins.signal:
                        c += 1
                        ins.sem = eng_sem[e]
                        ins.val = c
            for s, lst in self.dma_slots.items():
                c = 0
                for ins in lst:
                    c += 16
                    ins.sem = slot_sem[s]
                    ins.val = c
                    ins.signal = True
            block = es.enter_context(nc.Block())

            def run_stream(e, engobj):
                waited = {}
                for ins in self.streams[e]:
                    need = {}
                    for d in ins.deps:
                        k = id(d.sem)
                        if waited.get(k, 0) >= d.val:
                            continue
                        if k not in need or need[k][1] < d.val:
                            need[k] = (d.sem, d.val)
                    for k, (sem, val) in need.items():
                        engobj.wait_ge(sem, val)
                        waited[k] = val
                    bi = ins.fn(engobj)
                    if ins.signal:
                        bi.then_inc(ins.sem, 16 if ins.dma is not None else 1)
                if e == "sp":
                    fw = {}
                    for f in final_waits:
                        f = self.dma_slots[f.dma][-1]
                        fw[id(f.sem)] = f
                    for f in fw.values():
                        engobj.wait_ge(f.sem, f.val)

            @block.tensor
            def _(eng):
                run_stream("pe", eng)

            @block.scalar
            def _(eng):
                run_stream("act", eng)

            @block.vector
            def _(eng):
                run_stream("dve", eng)

            @block.gpsimd
            def _(eng):
                run_stream("pool", eng)

            @block.sync
            def _(eng):
                run_stream("sp", eng)


class Arena:
    def __init__(self, handle, nbytes):
        self.h = handle
        self.n = nbytes
        self.off = 0

    def mark(self):
        return self.off

    def release(self, m):
        self.off = m

    def alloc(self, shape, dt):
        esz = 4 if dt == F32 else 2
        nel = 1
        for s in shape[1:]:
            nel *= s
        nb = nel * esz
        self.off = (self.off + 31) // 32 * 32
        assert self.off + nb <= self.n, ("arena overflow", self.off, nb, self.n)
        a = self.h[:, self.off // 2:(self.off + nb) // 2]
        self.off += nb
        if dt == F32:
            a = a.bitcast(F32)
        if len(shape) == 3:
            a = a.rearrange("p (a b) -> p a b", a=shape[1])
        elif len(shape) == 4:
            a = a.rearrange("p (a b c) -> p a b c", a=shape[1], b=shape[2])
        return a


C_IDENT, C_ONES, C_U128, C_SL128, C_MSL, C_U16, C_O16, C_IND8 = 0, 128, 256, 384, 512, 640, 768, 896
NCONST = 904


def make_consts():
    c = np.zeros((128, NCONST), np.float32)
    i = np.arange(128)
    c[:, C_IDENT:C_IDENT + 128] = np.eye(128)
    c[:, C_ONES:C_ONES + 128] = 1.0
    c[:, C_U128:C_U128 + 128] = (i[:, None] <= i[None, :])
    c[:, C_SL128:C_SL128 + 128] = (i[:, None] > i[None, :])
    c[:, C_MSL:C_MSL + 128] = (i[:, None] > i[None, :])
    same = (i[:, None] // 16) == (i[None, :] // 16)
    c[:, C_U16:C_U16 + 128] = same & (i[:, None] <= i[None, :])
    c[:, C_O16:C_O16 + 128] = same & (i[:, None] > i[None, :])
    c[:, C_IND8:C_IND8 + 8] = (i[:, None] // 16) == np.arange(8)[None, :]
    return c


def chunk_cols(w, ncols_chunks):
    k, n = w.shape
    assert k == D and n == ncols_chunks * 128
    return np.ascontiguousarray(w.reshape(NCH, 128, ncols_chunks, 128).transpose(2, 1, 0, 3))


def pvec(v):
    v = np.asarray(v, np.float32)
    n = v.shape[-1] // 128
    v = v.reshape(v.shape[:-1] + (n, 128))
    return np.ascontiguousarray(np.moveaxis(v, -1, 0))


class Builder:
    def __init__(self, plan, do_final_norm=True):
        self.do_final_norm = do_final_norm
        self.NB = 2
        self.plan = plan
        self.nc = bass.Bass("TRN2", target_bir_lowering=False)
        self.P = Prog(self.nc)
        self.dram = {}

    def din(self, name, shape, dt=F32):
        t = self.nc.dram_tensor(name, list(shape), dt, kind="ExternalInput").ap()
        self.dram[name] = t
        return t

    def bank(self, b, n=512, off=0):
        return self.psum[:, b * 512 + off: b * 512 + off + n]

    def cst(self, off, n=128):
        return self.consts[:, off:off + n]

    def build(self):
        nc, P = self.nc, self.P
        with contextlib.ExitStack() as es:
            self.es = es
            d = self.din
            xT_d = d("xT", [NCH, 128, T])
            consts_d = d("consts", [128, NCONST])
            gains_d = d("gains", [128, 9 * NCH])
            fcw_d = d("ffn_cw", [128, DEPTH * NFC * 4])
            self.wup_d = d("ffn_wup", [DEPTH, 2 * NFC, 128, NCH * 128])
            self.wdn_d = d("ffn_wdn", [DEPTH, NFC, 128, D])
            self.declare_mixer_inputs()
            self.out_d = nc.dram_tensor("yT", [NCH, 128, T], F32, kind="ExternalOutput").ap()

            sb = lambda name, shape, dt: es.enter_context(nc.sbuf_tensor(name, shape, dt))
            self.xT = sb("xTs", [128, NCH, T], F32)
            self.consts = sb("consts_s", [128, NCONST], F32)
            self.cbf = sb("cbf", [128, 256], BF16)
            self.gains = sb("gains_s", [128, 9 * NCH], F32)
            self.fcw = sb("fcw_s", [128, DEPTH * NFC * 4], F32)
            self.eps_t = sb("eps_t", [128, 2], F32)
            ARENA_BYTES = 136 * 1024
            arena_h = sb("arena", [128, ARENA_BYTES // 2], BF16)
            self.arena = Arena(arena_h, ARENA_BYTES)
            self.psum = es.enter_context(nc.psum_tensor("psum", [128, 4096], F32))

            self.b_x = [[Buf("x%d_%d" % (c, tb)) for tb in range(16)] for c in range(NCH)]
            self.b_bk = [Buf("bank%d" % b, excl=True) for b in range(8)]
            self.b_const = Buf("const")
            self.bank_rr = 0

            xT = self.xT
            for c in range(NCH):
                P.op("sp", lambda e, c=c: e.dma_start(out=xT[:, c, :], in_=xT_d[c]),
                     writes=self.b_x[c], dma="x%d" % (c % 4))
            P.op("sp", lambda e: e.dma_start(out=self.consts[:], in_=consts_d), writes=[self.b_const], dma="c0")
            P.op("sp", lambda e: e.dma_start(out=self.gains[:], in_=gains_d), writes=[self.b_const], dma="c0")
            P.op("sp", lambda e: e.dma_start(out=self.fcw[:], in_=fcw_d), writes=[self.b_const], dma="c0")
            self.load_mixer_consts()
            P.op("dve", lambda e: e.tensor_copy(out=self.cbf[:, 0:256], in_=self.consts[:, 0:256]),
                 reads=[self.b_const], writes=[self.b_const])
            P.op("dve", lambda e: e.memset(self.eps_t[:, 0:1], EPS), writes=[self.b_const])
            P.op("dve", lambda e: e.memset(self.eps_t[:, 1:2], 1.0), writes=[self.b_const])
            self.eps_ap = self.eps_t[:, 0:1]
            self.one_ap = self.eps_t[:, 1:2]
            self.ident_bf = self.cbf[:, 0:128]
            self.ones_bf = self.cbf[:, 128:256]
            self.ident_f = self.cst(C_IDENT)
            self.ones_f = self.cst(C_ONES)
            P.fence()

            for kind, l in self.plan:
                if kind == "ffn":
                    self.ffn(l)
                elif kind == "odd":
                    self.odd_mixer(l)
                elif kind == "even":
                    self.even_mixer(l)
                P.fence()
            self.final_norm()
            P.emit(final_waits=self.final_dmas)
        return nc

    def bx(self, c, t0, n):
        return self.b_x[c][t0 // 128:(t0 + n) // 128]

    def bb(self, b):
        return [self.b_bk[b]]

    def next_bank(self):
        b = self.bank_rr
        self.bank_rr = (b + 1) % 8
        return b

    def norm_block(self, tb, gcol, dst, b_dst, sq, b_sq, rs, b_rs):
        P, xT = self.P, self.xT
        ts = slice(tb * 512, (tb + 1) * 512)
        for c in range(NCH):
            P.op("act", lambda e, c=c: e.activation(out=sq[:, c, :], in_=xT[:, c, ts], func=AF.Square),
                 reads=[*self.bx(c, tb * 512, 512)], writes=[b_sq[c]])
        bi = self.next_bank()
        bk = self.bank(bi)
        for c in range(NCH):
            P.op("pe", lambda e, c=c: e.matmul(bk, lhsT=self.ones_bf, rhs=sq[:, c, :], start=(c == 0), stop=(c == NCH - 1)),
                 reads=[b_sq[c], self.b_const], writes=self.bb(bi))
        P.op("act", lambda e: e.activation(out=rs, in_=bk, func=AF.Sqrt, scale=1.0 / D, bias=self.eps_ap),
             reads=self.bb(bi) + [self.b_const], writes=[b_rs])
        P.op("dve", lambda e: e.reciprocal(out=rs, in_=rs), reads=[b_rs], writes=[b_rs])
        for c in range(NCH):
            g = self.gains[:, gcol + c: gcol + c + 1]
            P.op("dve", lambda e, c=c, g=g: e.scalar_tensor_tensor(out=dst(c), in0=xT[:, c, ts], scalar=g, in1=rs,
                                                                  op0=ALU.mult, op1=ALU.mult),
                 reads=[*self.bx(c, tb * 512, 512), b_rs, self.b_const], writes=b_dst(c))

    def ffn(self, l):
        P, A, xT = self.P, self.arena, self.xT
        m0 = A.mark()
        hT = A.alloc([128, NCH, T], BF16)
        b_h = [[Buf("h%d_%d" % (c, tb)) for tb in range(4)] for c in range(NCH)]
        wup = [A.alloc([128, 2, NCH * 128], BF16) for _ in range(3)]
        b_wup = [Buf("wup%d" % s) for s in range(3)]
        G = 11
        wdn = A.alloc([128, G, D], BF16)
        b_wdn = [Buf("wdn%d" % i) for i in range(G)]
        m1 = A.mark()

        jstate = {"n": 0}

        def load_wup(j):
            s = jstate["n"] % 3
            jstate["n"] += 1
            for gv in range(2):
                src = self.wup_d[l, gv * NFC + j]
                P.op("pool", lambda e, s=s, gv=gv, src=src: e.dma_start(out=wup[s][:, gv, :], in_=src),
                     writes=[b_wup[s]], dma="wup%d" % s)
            return s

        def load_wdn(g):
            for i in range(G):
                src = self.wdn_d[l, g * G + i]
                P.op("pool", lambda e, i=i, src=src: e.dma_start(out=wdn[:, i, :], in_=src),
                     writes=[b_wdn[i]], dma="wdn%d" % (i % 2))

        slots = {}
        slots[0] = load_wup(0)
        slots[1] = load_wup(1)
        load_wdn(0)

        sq = [A.alloc([128, NCH, 512], BF16) for _ in range(2)]
        rs = [A.alloc([128, 512], F32) for _ in range(2)]
        b_sq = [[Buf("sq") for c in range(NCH)] for _ in range(2)]
        b_rs = [Buf("rs") for _ in range(2)]
        for tb in range(4):
            ts = slice(tb * 512, (tb + 1) * 512)
            self.norm_block(tb, (4 + l) * NCH, lambda c, ts=ts: hT[:, c, ts], lambda c, tb=tb: [b_h[c][tb]],
                            sq[tb % 2], b_sq[tb % 2], rs[tb % 2], b_rs[tb % 2])
        A.release(m1)
        P.fence()

        act = A.alloc([128, G, T], BF16)
        b_act = [[Buf("act") for h in range(2)] for i in range(G)]
        ctmp = [A.alloc([128, 1024], F32) for _ in range(2)]
        stmp = [A.alloc([128, 1024], BF16) for _ in range(2)]
        b_ct = [Buf("ct") for _ in range(2)]
        b_st = [Buf("st") for _ in range(2)]
        halo = A.alloc([128, 2], F32)
        b_halo = Buf("halo")
        cw = lambda j, k: self.fcw[:, (l * NFC + j) * 4 + k: (l * NFC + j) * 4 + k + 1]

        step = 0
        for g in range(2):
            for jj in range(G):
                j = g * G + jj
                if j + 2 < NFC:
                    slots[j + 2] = load_wup(j + 2)
                s = slots[j]
                for half in range(2):
                    ps = step % 2
                    step += 1
                    gb = [4 * ps, 4 * ps + 1]
                    vb = [4 * ps + 2, 4 * ps + 3]
                    Gp = self.psum[:, ps * 2048: ps * 2048 + 1024]
                    Vp = self.psum[:, ps * 2048 + 1024: ps * 2048 + 2048]
                    for gv, banks in ((0, gb), (1, vb)):
                        for t2 in range(2):
                            tb = half * 2 + t2
                            out = self.bank(banks[t2])
                            for k in range(NCH):
                                P.op("pe", lambda e, out=out, s=s, gv=gv, k=k, tb=tb: e.matmul(
                                    out, lhsT=wup[s][:, gv, k * 128:(k + 1) * 128], rhs=hT[:, k, tb * 512:(tb + 1) * 512],
                                    start=(k == 0), stop=(k == NCH - 1)),
                                    reads=[b_wup[s], b_h[k][tb]], writes=self.bb(banks[t2]))
                    ct, st = ctmp[ps], stmp[ps]
                    bG = self.bb(gb[0]) + self.bb(gb[1])
                    bV = self.bb(vb[0]) + self.bb(vb[1])
                    P.op("act", lambda e, ct=ct, Gp=Gp, j=j: e.activation(out=ct, in_=Gp, func=AF.Identity,
                                                                          scale=cw(j, 2), bias=cw(j, 3)),
                         reads=bG + [self.b_const], writes=[b_ct[ps]])
                    P.op("dve", lambda e, ct=ct, Gp=Gp, j=j: e.scalar_tensor_tensor(
                        out=ct[:, 1:1024], in0=Gp[:, 0:1023], scalar=cw(j, 1), in1=ct[:, 1:1024], op0=ALU.mult, op1=ALU.add),
                        reads=bG + [b_ct[ps]], writes=[b_ct[ps]])
                    P.op("dve", lambda e, ct=ct, Gp=Gp, j=j: e.scalar_tensor_tensor(
                        out=ct[:, 2:1024], in0=Gp[:, 0:1022], scalar=cw(j, 0), in1=ct[:, 2:1024], op0=ALU.mult, op1=ALU.add),
                        reads=bG + [b_ct[ps]], writes=[b_ct[ps]])
                    if half == 1:
                        P.op("dve", lambda e, ct=ct, j=j: e.scalar_tensor_tensor(
                            out=ct[:, 0:1], in0=halo[:, 1:2], scalar=cw(j, 1), in1=ct[:, 0:1], op0=ALU.mult, op1=ALU.add),
                            reads=[b_halo, b_ct[ps]], writes=[b_ct[ps]])
                        P.op("dve", lambda e, ct=ct, j=j: e.scalar_tensor_tensor(
                            out=ct[:, 0:2], in0=halo[:, 0:2], scalar=cw(j, 0), in1=ct[:, 0:2], op0=ALU.mult, op1=ALU.add),
                            reads=[b_halo, b_ct[ps]], writes=[b_ct[ps]])
                    else:
                        P.op("act", lambda e, Gp=Gp: e.activation(out=halo, in_=Gp[:, 1022:1024], func=AF.Identity),
                             reads=bG, writes=[b_halo])
                    P.op("act", lambda e, ct=ct, st=st: e.activation(out=st, in_=ct, func=AF.Silu),
                         reads=[b_ct[ps]], writes=[b_st[ps]])
                    P.op("dve", lambda e, st=st, Vp=Vp, jj=jj, half=half: e.tensor_tensor(
                        out=act[:, jj, half * 1024:(half + 1) * 1024], in0=Vp, in1=st, op=ALU.mult),
                        reads=bV + [b_st[ps]], writes=[b_act[jj][half]])
            for dch in range(NCH):
                for tb in range(4):
                    bi = self.next_bank()
                    bk = self.bank(bi)
                    for i in range(G):
                        P.op("pe", lambda e, bk=bk, i=i, dch=dch, tb=tb: e.matmul(
                            bk, lhsT=wdn[:, i, dch * 128:(dch + 1) * 128], rhs=act[:, i, tb * 512:(tb + 1) * 512],
                            start=(i == 0), stop=(i == G - 1)),
                            reads=[b_wdn[i], b_act[i][tb // 2]], writes=self.bb(bi))
                    P.op("dve", lambda e, bk=bk, dch=dch, tb=tb: e.tensor_tensor(
                        out=xT[:, dch, tb * 512:(tb + 1) * 512], in0=bk, in1=xT[:, dch, tb * 512:(tb + 1) * 512], op=ALU.add),
                        reads=self.bb(bi) + self.bx(dch, tb * 512, 512), writes=self.bx(dch, tb * 512, 512))
            if g == 0:
                load_wdn(1)
        A.release(m0)

    def final_norm(self):
        P, A, xT = self.P, self.arena, self.xT
        m0 = A.mark()
        sq = [A.alloc([128, NCH, 512], BF16) for _ in range(2)]
        rs = [A.alloc([128, 512], F32) for _ in range(2)]
        b_sq = [[Buf("sq") for c in range(NCH)] for _ in range(2)]
        b_rs = [Buf("rs") for _ in range(2)]
        for tb in range(4 if self.do_final_norm else 0):
            ts = slice(tb * 512, (tb + 1) * 512)
            self.norm_block(tb, 8 * NCH, lambda c, ts=ts: xT[:, c, ts], lambda c, tb=tb: self.bx(c, tb * 512, 512),
                            sq[tb % 2], b_sq[tb % 2], rs[tb % 2], b_rs[tb % 2])
        self.final_dmas = []
        for c in range(NCH):
            ins = P.op("sp", lambda e, c=c: e.dma_start(out=self.out_d[c], in_=xT[:, c, :]),
                       reads=self.b_x[c], dma="out%d" % (c % 4))
            self.final_dmas.append(ins)
        A.release(m0)

    def declare_mixer_inputs(self):
        d = self.din
        self.cwin_d = d("c_win", [2, 16, 128, NCH * 128])
        self.cgw_d = d("c_gw", [2, 2, 4, 256, 256])
        self.cwout_d = d("c_wout", [2, NCH, 128, D])
        self.cpv_d = d("c_pv", [128, 2 * NCH * 8])
        self.declare_even_inputs()

    def load_mixer_consts(self):
        P = self.P
        self.cpv = self.es.enter_context(self.nc.sbuf_tensor("cpv_s", [128, 2 * NCH * 8], F32))
        P.op("sp", lambda e: e.dma_start(out=self.cpv[:], in_=self.cpv_d), writes=[self.b_const], dma="c0")
        self.load_even_consts()

    def odd_mixer(self, l):
        P, A, xT = self.P, self.arena, self.xT
        j = l // 2
        m0 = A.mark()
        win = A.alloc([128, 16, NCH * 128], BF16)
        b_win = [Buf("win") for _ in range(16)]
        gw = A.alloc([128, 16, 256], BF16)
        b_gw = Buf("gw")
        wout = A.alloc([128, NCH, D], BF16)
        b_wout = [Buf("wout") for _ in range(NCH)]
        for c in range(16):
            P.op("pool", lambda e, c=c: e.dma_start(out=win[:, c, :], in_=self.cwin_d[j, c]),
                 writes=[b_win[c]], dma="mw%d" % (c % 2))
        for ax in range(2):
            for hd in range(4):
                src = self.cgw_d[j, ax, hd].rearrange("(kc p) n -> p kc n", p=128)
                i0 = (ax * 4 + hd) * 2
                P.op("pool", lambda e, i0=i0, src=src: e.dma_start(out=gw[:, i0:i0 + 2, :], in_=src),
                     writes=[b_gw], dma="mw2")
        for k in range(NCH):
            P.op("pool", lambda e, k=k: e.dma_start(out=wout[:, k, :], in_=self.cwout_d[j, k]),
                 writes=[b_wout[k]], dma="mw%d" % (k % 2))

        pv = lambda c, k: self.cpv[:, (j * NCH + c) * 8 + k:(j * NCH + c) * 8 + k + 1]
        pvall = self.cpv[:, j * NCH * 8:(j + 1) * NCH * 8].rearrange("p (c k) -> p c k", k=8)
        der = A.alloc([128, 8, NCH], F32)
        b_der = Buf("der")
        lam = pvall[:, :, 7]
        R_, W_ = [self.b_const, b_der], [b_der]
        P.op("act", lambda e: e.activation(out=der[:, 0, :], in_=lam, func=AF.Exp, scale=-1.0), reads=R_, writes=W_)
        P.op("dve", lambda e: e.tensor_scalar(out=der[:, 1, :], in0=der[:, 0, :], scalar1=1.0, scalar2=None, op0=ALU.add), reads=R_, writes=W_)
        P.op("act", lambda e: e.activation(out=der[:, 2, :], in_=der[:, 1, :], func=AF.Ln), reads=R_, writes=W_)
        P.op("dve", lambda e: e.tensor_scalar(out=der[:, 3, :], in0=der[:, 1, :], scalar1=-1.0, scalar2=1e-30, op0=ALU.add, op1=ALU.max), reads=R_, writes=W_)
        P.op("dve", lambda e: e.reciprocal(out=der[:, 3, :], in_=der[:, 3, :]), reads=R_, writes=W_)
        P.op("dve", lambda e: e.tensor_tensor(out=der[:, 2, :], in0=der[:, 2, :], in1=der[:, 0, :], op=ALU.mult), reads=R_, writes=W_)
        P.op("dve", lambda e: e.tensor_tensor(out=der[:, 2, :], in0=der[:, 2, :], in1=der[:, 3, :], op=ALU.mult), reads=R_, writes=W_)
        P.op("dve", lambda e: e.tensor_scalar(out=der[:, 4, :], in0=der[:, 2, :], scalar1=-4.0, scalar2=None, op0=ALU.mult), reads=R_, writes=W_)
        P.op("dve", lambda e: e.tensor_scalar(out=der[:, 5, :], in0=der[:, 2, :], scalar1=-8.0, scalar2=None, op0=ALU.mult), reads=R_, writes=W_)
        P.op("dve", lambda e: e.tensor_scalar(out=der[:, 6, :], in0=pvall[:, :, 5], scalar1=0.5, scalar2=None, op0=ALU.mult), reads=R_, writes=W_)
        P.op("dve", lambda e: e.tensor_scalar(out=der[:, 7, :], in0=pvall[:, :, 6], scalar1=0.5, scalar2=None, op0=ALU.mult), reads=R_, writes=W_)
        dv = lambda k, c: der[:, k, c:c + 1]

        hTq = A.alloc([128, NCH, 512], BF16)
        b_hq = [Buf("hq") for _ in range(NCH)]
        sq = A.alloc([128, NCH, 512], BF16)
        b_sq = [Buf("sq") for _ in range(NCH)]
        rs = A.alloc([128, 512], F32)
        b_rs = Buf("rs")
        gate = A.alloc([128, NCH, 512], BF16)
        b_gate = [Buf("gate") for _ in range(NCH)]
        obf = A.alloc([128, NCH, 512], BF16)
        b_obf = [Buf("obf") for _ in range(NCH)]
        xc = [A.alloc([128, 2, 512], F32) for _ in range(2)]
        xcb = [A.alloc([128, 2, 512], BF16) for _ in range(2)]
        b_xc = [[Buf("xc") for _ in range(2)] for _ in range(2)]
        b_xcb = [[Buf("xcb") for _ in range(2)] for _ in range(2)]
        halo = [A.alloc([128, NCH, 4], F32) for _ in range(2)]
        b_halo = [[Buf("halo") for _ in range(NCH)] for _ in range(2)]
        carry = [A.alloc([128, NCH], F32) for _ in range(2)]
        b_carry = [[Buf("carry") for _ in range(NCH)] for _ in range(2)]
        NT = 5
        tmp = [[A.alloc([128, 512], F32) for _ in range(NT)] for _ in range(2)]
        b_tmp = [[Buf("tmp") for _ in range(NT)] for _ in range(2)]
        it = 0
        for q in range(4):
            self.norm_block(q, l * NCH, lambda c: hTq[:, c, :], lambda c: [b_hq[c]], sq, b_sq, rs, b_rs)
            for c in range(NCH):
                bi = self.next_bank()
                bk = self.bank(bi)
                for k in range(NCH):
                    P.op("pe", lambda e, bk=bk, c=c, k=k: e.matmul(bk, lhsT=win[:, c, k * 128:(k + 1) * 128], rhs=hTq[:, k, :],
                                                                 start=(k == 0), stop=(k == NCH - 1)),
                         reads=[b_win[c], b_hq[k]], writes=self.bb(bi))
                P.op("act", lambda e, bk=bk, c=c: e.activation(out=gate[:, c, :], in_=bk, func=AF.Gelu_apprx_tanh),
                     reads=self.bb(bi), writes=[b_gate[c]])
            hcur, hprev = halo[q % 2], halo[(q + 1) % 2]
            bhcur, bhprev = b_halo[q % 2], b_halo[(q + 1) % 2]
            ccur, cprev = carry[q % 2], carry[(q + 1) % 2]
            bccur, bcprev = b_carry[q % 2], b_carry[(q + 1) % 2]
            for hd in range(4):
                xs = hd % 2
                for kc in range(2):
                    c = 2 * hd + kc
                    bi = self.next_bank()
                    bk = self.bank(bi)
                    for k in range(NCH):
                        P.op("pe", lambda e, bk=bk, c=c, k=k: e.matmul(bk, lhsT=win[:, 8 + c, k * 128:(k + 1) * 128], rhs=hTq[:, k, :],
                                                                     start=(k == 0), stop=(k == NCH - 1)),
                             reads=[b_win[8 + c], b_hq[k]], writes=self.bb(bi))
                    xcc = xc[xs][:, kc, :]
                    bx = b_xc[xs][kc]
                    P.op("act", lambda e, bk=bk, xcc=xcc, c=c: e.activation(out=xcc, in_=bk, func=AF.Identity, scale=pv(c, 3), bias=pv(c, 4)),
                         reads=self.bb(bi) + [self.b_const], writes=[bx])
                    P.op("act", lambda e, bk=bk, c=c, hcur=hcur: e.activation(out=hcur[:, c, 0:3], in_=bk[:, 509:512], func=AF.Identity),
                         reads=self.bb(bi), writes=[bhcur[c]])
                    for s in (1, 2, 3):
                        P.op("dve", lambda e, bk=bk, xcc=xcc, c=c, s=s: e.scalar_tensor_tensor(
                            out=xcc[:, s:512], in0=bk[:, 0:512 - s], scalar=pv(c, 3 - s), in1=xcc[:, s:512], op0=ALU.mult, op1=ALU.add),
                            reads=self.bb(bi) + [bx, self.b_const], writes=[bx])
                        if q > 0:
                            P.op("dve", lambda e, xcc=xcc, c=c, s=s, hprev=hprev: e.scalar_tensor_tensor(
                                out=xcc[:, 0:s], in0=hprev[:, c, 3 - s:3], scalar=pv(c, 3 - s), in1=xcc[:, 0:s], op0=ALU.mult, op1=ALU.add),
                                reads=[bhprev[c], bx, self.b_const], writes=[bx])
                    P.op("pool", lambda e, xcc=xcc, xs=xs, kc=kc: e.tensor_copy(out=xcb[xs][:, kc, :], in_=xcc),
                         reads=[bx], writes=[b_xcb[xs][kc]])
                for phase in range(3):
                    for jc in range(2):
                        oc = 2 * hd + jc
                        ts_ = jc
                        ta, tx, at, mt, hs = tmp[ts_]
                        bta, btx, bat, bmt, bhs = b_tmp[ts_]
                        if phase == 0:
                            banks = []
                            for ax in range(2):
                                bi = self.next_bank()
                                bk = self.bank(bi)
                                banks.append((bi, bk))
                                for kc in range(2):
                                    gi = (ax * 4 + hd) * 2 + kc
                                    P.op("pe", lambda e, bk=bk, gi=gi, jc=jc, xs=xs, kc=kc: e.matmul(
                                        bk, lhsT=gw[:, gi, jc * 128:(jc + 1) * 128], rhs=xcb[xs][:, kc, :], start=(kc == 0), stop=(kc == 1)),
                                        reads=[b_gw, b_xcb[xs][kc]], writes=self.bb(bi))
                            (bia, bka), (bix, bkx) = banks
                            self.act(ta, bka, AF.Tanh, self.bb(bia) + [b_der], [bta], scale=0.5, bias=dv(6, oc))
                            self.act(tx, bkx, AF.Tanh, self.bb(bix) + [b_der], [btx], scale=0.5, bias=dv(7, oc))
                            self.act(at, ta, AF.Exp, [bta, b_der], [bat], scale=dv(4, oc), bias=dv(4, oc))
                            self.act(mt, ta, AF.Exp, [bta, b_der], [bmt], scale=dv(5, oc), bias=dv(5, oc))
                            self.act(mt, mt, AF.Relu, [bmt, self.b_const], [bmt], scale=-1.0, bias=self.one_ap)
                        elif phase == 1:
                            self.act(mt, mt, AF.Sqrt, [bmt], [bmt])
                        else:
                            xcc = xc[xs][:, jc, :]
                            self.stt(tx, tx, 1.0, xcc, ALU.add, ALU.mult, [btx, b_xc[xs][jc]], [btx])
                            self.stt(tx, tx, 0.5, mt, ALU.mult, ALU.mult, [btx, bmt], [btx])
                            init = 0.0 if q == 0 else cprev[:, oc:oc + 1]
                            P.op("dve", lambda e, hs=hs, at=at, tx=tx, init=init: e.tensor_tensor_scan(
                                out=hs, data0=at, data1=tx, initial=init, op0=ALU.mult, op1=ALU.add),
                                reads=[bat, btx] + ([bcprev[oc]] if q > 0 else []), writes=[bhs])
                            self.act(ccur[:, oc:oc + 1], hs[:, 511:512], AF.Identity, [bhs], [bccur[oc]])
                            self.tt("pool", obf[:, oc, :], hs, gate[:, oc, :], ALU.mult, [bhs, b_gate[oc]], [b_obf[oc]])
            for dch in range(NCH):
                bi = self.next_bank()
                bk = self.bank(bi)
                for k in range(NCH):
                    P.op("pe", lambda e, bk=bk, dch=dch, k=k: e.matmul(bk, lhsT=wout[:, k, dch * 128:(dch + 1) * 128], rhs=obf[:, k, :],
                                                                     start=(k == 0), stop=(k == NCH - 1)),
                         reads=[b_wout[k], b_obf[k]], writes=self.bb(bi))
                P.op("dve", lambda e, bk=bk, dch=dch, q=q: e.tensor_tensor(
                    out=xT[:, dch, q * 512:(q + 1) * 512], in0=bk, in1=xT[:, dch, q * 512:(q + 1) * 512], op=ALU.add),
                    reads=self.bb(bi) + self.bx(dch, q * 512, 512), writes=self.bx(dch, q * 512, 512))
        A.release(m0)


    def act(self, out, in_, func, r, w, **kw):
        return self.P.op("act", lambda e: e.activation(out=out, in_=in_, func=func, **kw), reads=r, writes=w)

    def mm(self, out, lhsT, rhs, r, w, start=True, stop=True, **kw):
        return self.P.op("pe", lambda e: e.matmul(out, lhsT=lhsT, rhs=rhs, start=start, stop=stop, **kw), reads=r, writes=w)

    def tr(self, out, in_, ident, r, w):
        return self.P.op("pe", lambda e: e.transpose(out=out, in_=in_, identity=ident), reads=r, writes=w)

    def tt(self, eng, out, in0, in1, op, r, w):
        return self.P.op(eng, lambda e: e.tensor_tensor(out=out, in0=in0, in1=in1, op=op), reads=r, writes=w)

    def ts(self, eng, out, in0, s1, s2, op0, op1, r, w):
        if s2 is None:
            return self.P.op(eng, lambda e: e.tensor_scalar(out=out, in0=in0, scalar1=s1, scalar2=None, op0=op0), reads=r, writes=w)
        return self.P.op(eng, lambda e: e.tensor_scalar(out=out, in0=in0, scalar1=s1, scalar2=s2, op0=op0, op1=op1), reads=r, writes=w)

    def stt(self, out, in0, scalar, in1, op0, op1, r, w):
        return self.P.op("dve", lambda e: e.scalar_tensor_tensor(out=out, in0=in0, scalar=scalar, in1=in1, op0=op0, op1=op1),
                         reads=r, writes=w)

    def cp(self, eng, out, in_, r, w):
        return self.P.op(eng, lambda e: e.tensor_copy(out=out, in_=in_), reads=r, writes=w)

    def recip(self, out, in_, r, w):
        return self.P.op("dve", lambda e: e.reciprocal(out=out, in_=in_), reads=r, writes=w)

    def qtile(self, i):
        return self.psum[:, i * 128:(i + 1) * 128]

    def declare_even_inputs(self):
        d = self.din
        self.abfm_d = d("ab_fm", [2, 20, 128, NCH * 128])
        self.abtm_d = d("ab_tm", [2, 128, NCH * 1544])
        self.abwout_d = d("ab_wout", [2, NCH, 128, D])
        self.epv_d = d("e_pv", [128, 2 * 50])
        self.ebc_d = d("e_bc", [1040])

    def load_even_consts(self):
        P = self.P
        self.epv = self.es.enter_context(self.nc.sbuf_tensor("epv_s", [128, 100], F32))
        self.ebc = self.es.enter_context(self.nc.sbuf_tensor("ebc_s", [128, 16], F32))
        P.op("sp", lambda e: e.dma_start(out=self.epv[:], in_=self.epv_d), writes=[self.b_const], dma="c0")
        P.op("sp", lambda e: e.dma_start(out=self.ebc[:], in_=self.ebc_d[0:16].partition_broadcast(128)),
             writes=[self.b_const], dma="c0")

    def even_mixer(self, l):
        P, A, xT = self.P, self.arena, self.xT
        j = l // 2
        NB = self.NB
        TQ = NB * 128
        NQ = T // TQ
        m0 = A.mark()
        C = self.cst
        U128, SL128, MSL, U16, O16, IND8 = C(C_U128), C(C_SL128), C(C_MSL), C(C_U16), C(C_O16), C(C_IND8, 8)
        bc = self.b_const
        wfm = [A.alloc([128, NCH * 128], BF16) for _ in range(3)]
        b_wfm = [Buf("wfm") for _ in range(3)]
        wtm = A.alloc([128, NCH, 1544], BF16)
        b_wtm = [Buf("wtm") for _ in range(NCH)]
        wout = A.alloc([128, NCH, D], BF16)
        b_wout = [Buf("wout") for _ in range(NCH)]
        wtm_src = self.abtm_d[j].rearrange("p (k f) -> p k f", k=NCH)
        for k in range(NCH):
            P.op("pool", lambda e, k=k: e.dma_start(out=wtm[:, k, :], in_=wtm_src[:, k, :]), writes=[b_wtm[k]], dma="mw%d" % (k % 2))
        for k in range(NCH):
            P.op("pool", lambda e, k=k: e.dma_start(out=wout[:, k, :], in_=self.abwout_d[j, k]), writes=[b_wout[k]], dma="mw%d" % (k % 2))
        fm_list = [(qi, ci) for qi in range(NQ) for ci in range(20)]
        fm_state = {"n": 0}

        def load_fm():
            n = fm_state["n"]
            if n >= len(fm_list):
                return
            fm_state["n"] += 1
            ci = fm_list[n][1]
            s = n % 3
            P.op("pool", lambda e: e.dma_start(out=wfm[s], in_=self.abfm_d[j, ci]), writes=[b_wfm[s]], dma="wfm%d" % s)

        load_fm()
        load_fm()
        lay = A.alloc([128, 16], F32)
        b_lay = Buf("lay")
        self.act(lay[:, 0:4], self.ebc[:, j * 4:j * 4 + 4], AF.Exp, [bc], [b_lay])
        self.ts("dve", lay[:, 0:4], lay[:, 0:4], -1.0, None, ALU.mult, None, [b_lay], [b_lay])
        self.cp("dve", lay[:, 4:8], self.ebc[:, 8 + j * 4:8 + j * 4 + 4], [bc], [b_lay])
        nA, dtb = lay[:, 0:4], lay[:, 4:8]
        lb = A.alloc([128, 512], F32)
        oml = A.alloc([128, 512], F32)
        b_lb = Buf("lb")
        m1 = A.mark()
        raw = A.alloc([128, 2, 512], F32)
        tmpl = A.alloc([128, 4, 512], F32)
        b_raw = Buf("raw")
        P.op("sp", lambda e: e.dma_start(out=raw.rearrange("p a b -> p (a b)"), in_=self.ebc_d[16:1040].partition_broadcast(128)),
             writes=[b_raw], dma="c0")
        R_, W_ = [b_raw, b_lb], [b_raw, b_lb]
        mx, e0, e1, s_ = tmpl[:, 0, :], tmpl[:, 1, :], tmpl[:, 2, :], tmpl[:, 3, :]
        self.tt("dve", mx, raw[:, 0, :], raw[:, 1, :], ALU.max, R_, W_)
        self.tt("dve", e0, raw[:, 0, :], mx, ALU.subtract, R_, W_)
        self.tt("dve", e1, raw[:, 1, :], mx, ALU.subtract, R_, W_)
        self.act(e0, e0, AF.Exp, R_, W_)
        self.act(e1, e1, AF.Exp, R_, W_)
        self.tt("dve", s_, e0, e1, ALU.add, R_, W_)
        self.recip(s_, s_, R_, W_)
        self.tt("dve", e0, e0, s_, ALU.mult, R_, W_)
        self.tt("dve", e1, e1, s_, ALU.mult, R_, W_)
        if j == 0:
            self.tt("dve", lb, e0, e0, ALU.subtract, R_, W_)
        else:
            self.tt("dve", mx, e0, e1, ALU.add, R_, W_)
            self.tt("dve", lb, mx, e0, ALU.subtract, R_, W_)
        self.ts("dve", oml, lb, -1.0, 1.0, ALU.mult, ALU.add, R_, W_)
        A.release(m1)
        P.fence()
        pvq = lambda ci, k: self.epv[:, j * 50 + ci * 4 + k:j * 50 + ci * 4 + k + 1]
        gnorm = self.epv[:, j * 50 + 48:j * 50 + 49]
        hnorm = self.epv[:, j * 50 + 49:j * 50 + 50]

        H0 = 4
        hTq = A.alloc([128, NCH, TQ + H0], BF16)
        b_hq = [Buf("hq") for _ in range(NCH)]
        b_hh = Buf("hh")
        oTq = A.alloc([128, NCH, TQ], BF16)
        b_oq = [Buf("oq") for _ in range(NCH)]
        rs = A.alloc([128, TQ], F32)
        b_rs = Buf("rs")
        kT = A.alloc([128, 4, TQ], F32)
        kTb = A.alloc([128, 4, TQ], BF16)
        qT = A.alloc([128, 4, TQ], BF16)
        vT = A.alloc([128, 4, TQ], F32)
        zg = A.alloc([128, 4, TQ], BF16)
        gbg = A.alloc([128, 4, TQ], BF16)
        b_kT = [Buf("kT") for _ in range(4)]
        b_kTb = [Buf("kTb") for _ in range(4)]
        b_qT = [Buf("qT") for _ in range(4)]
        b_vT = [Buf("vT") for _ in range(4)]
        b_zg = [Buf("zg") for _ in range(4)]
        b_gbg = [Buf("gbg") for _ in range(4)]
        ctmp = [A.alloc([128, TQ], F32) for _ in range(2)]
        b_ct = [Buf("ct") for _ in range(2)]
        ct2 = A.alloc([128, TQ], F32)
        b_ct2 = Buf("ct2")
        qsl = A.alloc([128, NB, 512], F32)
        fraw = A.alloc([128, NB, 512], F32)
        vbf = A.alloc([128, NB, 512], BF16)
        baraw = A.alloc([128, NB, 8], F32)
        b_qsl = [Buf("qsl") for _ in range(NB)]
        b_fraw = [Buf("fraw") for _ in range(NB)]
        b_vbf = [Buf("vbf") for _ in range(NB)]
        b_ba = [Buf("ba") for _ in range(NB)]
        sqn = A.alloc([128, TQ], BF16)
        rn = ctmp[0]
        b_sqn, b_rn = Buf("sqn"), b_ct[0]
        NFT = 8
        gf = [[A.alloc([128, 128], F32) for _ in range(NFT)] for _ in range(4)]
        b_gf = [[Buf("gf") for _ in range(NFT)] for _ in range(4)]
        NBT = 9
        gb = [[A.alloc([128, 128], BF16) for _ in range(NBT)] for _ in range(4)]
        b_gb = [[Buf("gb") for _ in range(NBT)] for _ in range(4)]
        Sg = A.alloc([128, 4, 128], F32)
        Sgb = A.alloc([128, 4, 128], BF16)
        b_Sg = [Buf("Sg") for _ in range(4)]
        b_Sgb = [Buf("Sgb") for _ in range(4)]
        sc = [A.alloc([128, 48], F32) for _ in range(2)]
        b_sc = [Buf("sc") for _ in range(2)]
        Fh = A.alloc([128, 512], F32)
        KKh = A.alloc([128, 512], F32)
        EXh = A.alloc([128, 512], F32)
        kdh = A.alloc([128, 512], BF16)
        vmh = [A.alloc([128, 512], BF16) for _ in range(2)]
        qtTh = A.alloc([128, 512], BF16)
        ktTh = A.alloc([128, 512], BF16)
        scTh = A.alloc([128, 512], BF16)
        Sh = A.alloc([128, 512], F32)
        Shb = [A.alloc([128, 512], BF16) for _ in range(2)]
        dSh = A.alloc([128, 32], F32)
        b_F, b_KK, b_EX, b_kd = [Buf(n) for n in ("F", "KK", "EX", "kd")]
        b_vm = [Buf("vm") for _ in range(2)]
        b_qtT, b_ktT, b_dS = Buf("qtT"), Buf("ktT"), Buf("dS")
        b_scT = [Buf("scT") for _ in range(4)]
        b_Sh = [Buf("Sh") for _ in range(4)]
        b_Shb = [Buf("Shb") for _ in range(2)]
        for h in range(4):
            P.op("dve", lambda e, h=h: e.memset(Sg[:, h, :], 0.0), writes=[b_Sg[h]])
            P.op("dve", lambda e, h=h: e.memset(Sgb[:, h, :], 0.0), writes=[b_Sgb[h]])
            P.op("dve", lambda e, h=h: e.memset(Sh[:, h * 128:(h + 1) * 128], 0.0), writes=[b_Sh[h]])
        P.op("dve", lambda e: e.memset(Shb[0], 0.0), writes=[b_Shb[0]])
        shb_cur = {"i": 0}
        bk = self.b_bk
        prot = {"i": 0}
        PB = [7, 0, 1, 2, 3, 4]

        def next_half():
            b_ = PB[prot["i"] % len(PB)]
            prot["i"] += 1
            return self.psum[:, b_ * 512: (b_ + 1) * 512], [bk[b_]]

        hb5, hb6 = self.bank(5), self.bank(6)
        B5, B6 = [bk[5]] * 4, [bk[6]] * 4
        tmsel = {"i": 0}

        for qi in range(NQ):
            tok0 = qi * TQ
            tsl = slice(tok0, tok0 + TQ)
            xb = lambda c: self.bx(c, tok0, TQ)
            if qi == 0:
                P.op("pool", lambda e: e.memset(hTq[:, :, 0:H0], 0.0), writes=[b_hh])
            else:
                self.cp("pool", hTq[:, :, 0:H0], hTq[:, :, TQ:TQ + H0], list(b_hq) + [b_hh], [b_hh])
            for c in range(NCH):
                self.act(oTq[:, c, :], xT[:, c, tsl], AF.Square, xb(c), [b_oq[c]])
            psb, pb = next_half()
            ps = psb[:, 0:TQ]
            for c in range(NCH):
                self.mm(ps, self.ones_bf, oTq[:, c, :], [b_oq[c], bc], pb, start=(c == 0), stop=(c == NCH - 1))
            self.act(rs, ps, AF.Ln, pb + [bc], [b_rs], scale=1.0 / D, bias=self.eps_ap)
            self.act(rs, rs, AF.Exp, [b_rs], [b_rs], scale=-0.5)
            for c in range(NCH):
                g = self.gains[:, l * NCH + c: l * NCH + c + 1]
                self.stt(hTq[:, c, H0:H0 + TQ], xT[:, c, tsl], g, rs, ALU.mult, ALU.mult, xb(c) + [b_rs, bc, b_hh], [b_hq[c]])
            for ci in range(20):
                n = qi * 20 + ci
                s = n % 3
                load_fm()
                psb, pb = next_half()
                h = ci % 4
                if ci < 12:
                    ps = psb[:, 0:TQ + H0]
                    for k in range(NCH):
                        self.mm(ps, wfm[s][:, k * 128:(k + 1) * 128], hTq[:, k, 0:H0 + TQ], [b_wfm[s], b_hq[k], b_hh], pb,
                                start=(k == 0), stop=(k == NCH - 1))
                    ct, bct = ctmp[ci % 2], b_ct[ci % 2]
                    tap = lambda sft: ps[:, H0 - sft:H0 - sft + TQ]
                    self.act(ct, tap(0), AF.Identity, pb + [bc], [bct], scale=pvq(ci, 3))
                    for sft in (1, 2, 3):
                        self.stt(ct, tap(sft), pvq(ci, 3 - sft), ct, ALU.mult, ALU.add, pb + [bct, bc], [bct])
                    if ci < 4:
                        self.act(qT[:, h, :], ct, AF.Silu, [bct], [b_qT[h]])
                    elif ci < 8:
                        self.act(kT[:, h, :], ct, AF.Silu, [bct], [b_kT[h]])
                    else:
                        self.act(vT[:, h, :], ct, AF.Silu, [bct], [b_vT[h]])
                else:
                    ps = psb[:, 0:TQ]
                    for k in range(NCH):
                        self.mm(ps, wfm[s][:, k * 128:(k + 1) * 128], hTq[:, k, H0:H0 + TQ], [b_wfm[s], b_hq[k]], pb,
                                start=(k == 0), stop=(k == NCH - 1))
                    if ci < 16:
                        self.act(zg[:, h, :], ps, AF.Silu, pb, [b_zg[h]])
                    else:
                        self.act(gbg[:, h, :], ps, AF.Silu, pb, [b_gbg[h]])
            for b in range(NB):
                lo = b * 128
                for seg in range(3):
                    i = tmsel["i"]
                    tmsel["i"] = 1 - i
                    pst, pbt = (hb5, B5) if i == 0 else (hb6, B6)
                    for k in range(NCH):
                        self.mm(pst, hTq[:, k, H0 + lo:H0 + lo + 128], wtm[:, k, seg * 512:(seg + 1) * 512], [b_hq[k], b_wtm[k]], pbt,
                                start=(k == 0), stop=(k == NCH - 1))
                    if seg == 0:
                        self.act(qsl[:, b, :], pst, AF.Silu, pbt, [b_qsl[b]])
                    elif seg == 1:
                        self.act(fraw[:, b, :], pst, AF.Identity, pbt, [b_fraw[b]])
                    else:
                        self.act(vbf[:, b, :], pst, AF.Identity, pbt, [b_vbf[b]])
                i = tmsel["i"]
                tmsel["i"] = 1 - i
                pst, pbt = (hb5, B5) if i == 0 else (hb6, B6)
                for k in range(NCH):
                    self.mm(pst[:, 0:8], hTq[:, k, H0 + lo:H0 + lo + 128], wtm[:, k, 1536:1544], [b_hq[k], b_wtm[k]], pbt,
                            start=(k == 0), stop=(k == NCH - 1))
                self.act(baraw[:, b, :], pst[:, 0:8], AF.Identity, pbt, [b_ba[b]])
            for half_ in range(2):
                combos = [(h, isk) for h in range(2 * half_, 2 * half_ + 2) for isk in range(2)]
                for i, (h, isk) in enumerate(combos):
                    src, bsrc = (kT, b_kT) if isk else (qT, b_qT)
                    self.act(oTq[:, i, :], src[:, h, :], AF.Square, [bsrc[h]], [b_oq[i]])
                pss = []
                for i, (h, isk) in enumerate(combos):
                    psb, pb = next_half()
                    ps = psb[:, 0:TQ]
                    self.mm(ps, self.ones_bf, oTq[:, i, :], [b_oq[i], bc], pb)
                    pss.append((ps, pb))
                for i, (h, isk) in enumerate(combos):
                    ps, pb = pss[i]
                    self.act(ps, ps, AF.Ln, pb + [bc], pb, bias=self.eps_ap)
                    self.act(ps, ps, AF.Exp, pb, pb, scale=-0.5)
                for i, (h, isk) in enumerate(combos):
                    ps, pb = pss[i]
                    if isk:
                        self.tt("dve", kT[:, h, :], kT[:, h, :], ps, ALU.mult, [b_kT[h]] + pb, [b_kT[h]])
                        self.cp("pool", kTb[:, h, :], kT[:, h, :], [b_kT[h]], [b_kTb[h]])
                    else:
                        self.stt(qT[:, h, :], qT[:, h, :], float(128 ** -0.5), ps, ALU.mult, ALU.mult, [b_qT[h]] + pb, [b_qT[h]])
            def chain_gen(b):
                scb, bscb = sc[b % 2], b_sc[b % 2]
                beta, g_tm, gc_sb, eg, dch, ekd, beg = (scb[:, 0:4], scb[:, 4:8], scb[:, 8:12], scb[:, 12:16], scb[:, 16:20],
                                                       scb[:, 20:24], scb[:, 24:28])
                t_a, t_b = scb[:, 28:32], scb[:, 32:36]
                RW = ([bscb], [bscb])
                self.act(t_a, baraw[:, b, 0:4], AF.Exp, [b_ba[b], bscb], [bscb], scale=-1.0)
                self.ts("dve", t_a, t_a, 1.0, None, ALU.add, None, *RW)
                self.recip(beta, t_a, *RW)
                self.tt("dve", t_b, baraw[:, b, 4:8], dtb, ALU.add, [b_ba[b], b_lay, bscb], [bscb])
                yield
                self.act(t_b, t_b, AF.Exp, *RW)
                self.act(t_b, t_b, AF.Ln, [bscb, bc], [bscb], bias=self.one_ap)
                self.tt("dve", g_tm, t_b, nA, ALU.mult, [bscb, b_lay], [bscb])
                yield
                t0 = self.bank(0)
                self.mm(t0[:, 0:4], U128, g_tm, [bc, bscb], [bk[0]])
                self.mm(t0[:, 4:8], self.ones_f, g_tm, [bc, bscb], [bk[0]])
                self.act(gc_sb, t0[:, 0:4], AF.Identity, [bk[0], bscb], [bscb])
                self.act(eg, t0[:, 0:4], AF.Exp, [bk[0], bscb], [bscb])
                self.act(dch, t0[:, 4:8], AF.Exp, [bk[0], bscb], [bscb])
                self.tt("dve", ekd, t0[:, 4:8], gc_sb, ALU.subtract, [bk[0], bscb], [bscb])
                yield
                self.act(ekd, ekd, AF.Exp, *RW)
                self.tt("dve", beg, beta, eg, ALU.mult, *RW)

            for _ in chain_gen(0):
                pass
            for b in range(NB):
                lo = b * 128
                bsl = slice(lo, lo + 128)
                scb, bscb = sc[b % 2], b_sc[b % 2]
                beta, g_tm, gc_sb, eg, dch, ekd, beg = (scb[:, 0:4], scb[:, 4:8], scb[:, 8:12], scb[:, 12:16], scb[:, 16:20],
                                                       scb[:, 20:24], scb[:, 24:28])
                Lc = dict(locals())
                gens = [self.gdn_chunk(Lc), self.hgrn_block(Lc)]
                if b + 1 < NB:
                    gens.append(chain_gen(b + 1))
                while gens:
                    for gen in list(gens):
                        try:
                            next(gen)
                        except StopIteration:
                            gens.remove(gen)
            for dch_ in range(NCH):
                psb, pb = next_half()
                ps = psb[:, 0:TQ]
                for k in range(NCH):
                    self.mm(ps, wout[:, k, dch_ * 128:(dch_ + 1) * 128], oTq[:, k, :], [b_wout[k], b_oq[k]], pb,
                            start=(k == 0), stop=(k == NCH - 1))
                self.tt("dve", xT[:, dch_, tsl], ps, xT[:, dch_, tsl], ALU.add, pb + self.bx(dch_, tok0, TQ), self.bx(dch_, tok0, TQ))
        A.release(m0)

    def gdn_chunk(self, L):
        bk, bc = L["bk"], L["bc"]
        U128, SL128, MSL = L["U128"], L["SL128"], L["MSL"]
        kT, kTb, qT, vT, zg = L["kT"], L["kTb"], L["qT"], L["vT"], L["zg"]
        b_kT, b_kTb, b_qT, b_vT, b_zg = L["b_kT"], L["b_kTb"], L["b_qT"], L["b_vT"], L["b_zg"]
        gf, b_gf, gb, b_gb = L["gf"], L["b_gf"], L["gb"], L["b_gb"]
        Sg, Sgb, b_Sg, b_Sgb = L["Sg"], L["Sgb"], L["b_Sg"], L["b_Sgb"]
        bscb, bsl = L["bscb"], L["bsl"]
        beta, g_tm, eg, dch, ekd, beg = L["beta"], L["g_tm"], L["eg"], L["dch"], L["ekd"], L["beg"]
        oTq, b_oq, gnorm = L["oTq"], L["b_oq"], L["gnorm"]
        idf = self.ident_f
        HS = range(4)
        T_ = lambda h, s: self.psum[:, (1 + h) * 512 + s * 128: (1 + h) * 512 + (s + 1) * 128]
        B_ = lambda h: [bk[1 + h]]
        T0 = lambda h: self.psum[:, h * 128:(h + 1) * 128]
        B0 = [bk[0]]
        col = lambda a, h: a[:, h:h + 1]
        for h in HS:
            self.tr(T_(h, 0), kT[:, h, bsl], idf, [b_kT[h], bc], B_(h))
            self.tr(T_(h, 1), vT[:, h, bsl], idf, [b_vT[h], bc], B_(h))
            self.ts("dve", gf[h][7], U128, col(g_tm, h), None, ALU.mult, None, [bc, bscb], [b_gf[h][7]])
        for h in HS:
            self.act(gb[h][1], T_(h, 1), AF.Identity, B_(h) + [bscb], [b_gb[h][1]], scale=col(beta, h))
            self.act(gb[h][2], T_(h, 0), AF.Identity, B_(h) + [bscb], [b_gb[h][2]], scale=col(beg, h))
            self.ts("dve", gb[h][3], T_(h, 0), col(ekd, h), None, ALU.mult, None, B_(h) + [bscb], [b_gb[h][3]])
        yield
        for h in HS:
            kb = kT[:, h, bsl]
            self.mm(T_(h, 2), kb, kb, [b_kT[h]], B_(h))
            self.mm(T_(h, 3), gf[h][7], SL128, [b_gf[h][7], bc], B_(h))
            self.mm(T_(h, 0), SL128, gf[h][7], [b_gf[h][7], bc], B_(h))
            self.mm(T_(h, 1), self.ones_f, gf[h][7], [b_gf[h][7], bc], B_(h))
        for h in HS:
            self.act(gf[h][5], T_(h, 3), AF.Exp, B_(h), [b_gf[h][5]])
            self.act(gf[h][6], T_(h, 0), AF.Exp, B_(h), [b_gf[h][6]])
            self.act(gb[h][0], T_(h, 1), AF.Exp, B_(h), [b_gb[h][0]])
            self.tt("dve", gf[h][0], T_(h, 2), gf[h][5], ALU.mult, B_(h) + [b_gf[h][5]], [b_gf[h][0]])
            self.stt(gf[h][0], gf[h][0], col(beta, h), MSL, ALU.mult, ALU.mult, [b_gf[h][0], bscb, bc], [b_gf[h][0]])
            self.tt("pool", gb[h][4], qT[:, h, bsl], gb[h][0], ALU.mult, [b_qT[h], b_gb[h][0]], [b_gb[h][4]])
        yield
        for h in HS:
            self.tr(T_(h, 2), gf[h][0], idf, [b_gf[h][0], bc], B_(h))
            self.mm(T_(h, 3), kTb[:, h, bsl], qT[:, h, bsl], [b_kTb[h], b_qT[h]], B_(h))
        for h in HS:
            self.act(gf[h][2], T_(h, 2), AF.Identity, B_(h), [b_gf[h][2]])
            self.tt("dve", gf[h][4], idf, T_(h, 2), ALU.subtract, B_(h) + [bc], [b_gf[h][4]])
            self.tt("dve", gf[h][6], T_(h, 3), gf[h][6], ALU.mult, B_(h) + [b_gf[h][6]], [b_gf[h][6]])
            self.tt("pool", gb[h][5], gf[h][6], U128, ALU.mult, [b_gf[h][6], bc], [b_gb[h][5]])
        yield
        pi, pti = 0, 2
        for m in range(1, 7):
            npi, npti = 1 - pi, 5 - pti
            for h in HS:
                self.mm(T_(h, 0), gf[h][pti], gf[h][pi], [b_gf[h][pti], b_gf[h][pi]], B_(h))
                if m < 6:
                    self.mm(T_(h, 1), gf[h][pi], gf[h][pti], [b_gf[h][pti], b_gf[h][pi]], B_(h))
            for h in HS:
                self.act(gf[h][npi], T_(h, 0), AF.Identity, B_(h), [b_gf[h][npi]])
                if m < 6:
                    self.cp("dve", gf[h][npti], T_(h, 1), B_(h), [b_gf[h][npti]])
            yield
            for h in HS:
                self.mm(T_(h, 2), gf[h][npi], gf[h][4], [b_gf[h][npi], b_gf[h][4]], B_(h))
            for h in HS:
                self.tt("dve", gf[h][4], gf[h][4], T_(h, 2), ALU.add, B_(h) + [b_gf[h][4]], [b_gf[h][4]])
            pi, pti = npi, npti
            yield
        for h in HS:
            self.act(gb[h][8], gf[h][4], AF.Identity, [b_gf[h][4]], [b_gb[h][8]])
        for h in HS:
            self.mm(T_(h, 2), gb[h][8], gb[h][1], [b_gb[h][8], b_gb[h][1]], B_(h))
            self.mm(T_(h, 3), gb[h][2], gb[h][8], [b_gb[h][8], b_gb[h][2]], B_(h))
        for h in HS:
            self.act(gf[h][7], T_(h, 2), AF.Identity, B_(h), [b_gf[h][7]])
            self.cp("dve", gb[h][6], T_(h, 3), B_(h), [b_gb[h][6]])
        yield
        for h in HS:
            self.mm(T_(h, 0), gb[h][6], Sgb[:, h, :], [b_gb[h][6], b_Sgb[h]], B_(h))
        for h in HS:
            self.tt("dve", gb[h][7], gf[h][7], T_(h, 0), ALU.subtract, B_(h) + [b_gf[h][7]], [b_gb[h][7]])
        yield
        for h in HS:
            self.mm(T_(h, 1), Sgb[:, h, :], gb[h][4], [b_Sgb[h], b_gb[h][4]], B_(h), start=True, stop=False, skip_group_check=True)
            self.mm(T_(h, 1), gb[h][7], gb[h][5], [b_gb[h][7], b_gb[h][5]], B_(h), start=False, stop=True, skip_group_check=True)
            self.mm(T_(h, 3), gb[h][3], gb[h][7], [b_gb[h][3], b_gb[h][7]], B_(h))
        for h in HS:
            self.stt(Sg[:, h, :], Sg[:, h, :], col(dch, h), T_(h, 3), ALU.mult, ALU.add, B_(h) + [b_Sg[h], bscb], [b_Sg[h]])
            self.act(Sgb[:, h, :], Sg[:, h, :], AF.Identity, [b_Sg[h]], [b_Sgb[h]])
        yield
        for h in HS:
            self.act(gb[h][0], T_(h, 1), AF.Square, B_(h), [b_gb[h][0]])
        for h in HS:
            self.mm(T_(h, 2), self.ones_bf, gb[h][0], [bc, b_gb[h][0]], B_(h))
        yield
        for h in HS:
            self.act(gf[h][5], T_(h, 2), AF.Ln, B_(h) + [bc], [b_gf[h][5]], scale=1.0 / 128, bias=self.eps_ap)
            self.act(gf[h][5], gf[h][5], AF.Exp, [b_gf[h][5]], [b_gf[h][5]], scale=-0.5)
            self.stt(gf[h][6], T_(h, 1), gnorm, gf[h][5], ALU.mult, ALU.mult, B_(h) + [b_gf[h][5], bc], [b_gf[h][6]])
            self.tt("pool", oTq[:, h, bsl], gf[h][6], zg[:, h, bsl], ALU.mult, [b_gf[h][6], b_zg[h]], [b_oq[h]])

    def hgrn_block(self, L):
        P = self.P
        bc = L["bc"]
        U16, O16, IND8 = L["U16"], L["O16"], L["IND8"]
        b, bsl = L["b"], L["bsl"]
        qsl, fraw, vbf, b_qsl, b_fraw, b_vbf = L["qsl"], L["fraw"], L["vbf"], L["b_qsl"], L["b_fraw"], L["b_vbf"]
        lb, oml, b_lb = L["lb"], L["oml"], L["b_lb"]
        Fh, KKh, EXh, kdh = L["Fh"], L["KKh"], L["EXh"], L["kdh"]
        b_F, b_KK, b_EX, b_kd = L["b_F"], L["b_KK"], L["b_EX"], L["b_kd"]
        vmh, b_vm, qtTh, ktTh, scTh = L["vmh"], L["b_vm"], L["qtTh"], L["ktTh"], L["scTh"]
        b_qtT, b_ktT, b_scT, b_dS = L["b_qtT"], L["b_ktT"], L["b_scT"], L["b_dS"]
        Sh, Shb, dSh, b_Sh, b_Shb, shb_cur = L["Sh"], L["Shb"], L["dSh"], L["b_Sh"], L["b_Shb"], L["shb_cur"]
        gbg, b_gbg, oTq, b_oq, hnorm = L["gbg"], L["b_gbg"], L["oTq"], L["b_oq"], L["hnorm"]
        gf, b_gf, gb, b_gb = L["gf"], L["b_gf"], L["gb"], L["b_gb"]
        hb5, hb6, B5, B6 = L["hb5"], L["hb6"], L["B5"], L["B6"]
        idf = self.ident_f
        hc = lambda a, h: a[:, h * 128:(h + 1) * 128]
        self.act(Fh, fraw[:, b, :], AF.Exp, [b_fraw[b]], [b_F], scale=-1.0)
        self.ts("dve", Fh, Fh, 1.0, None, ALU.add, None, [b_F], [b_F])
        self.recip(Fh, Fh, [b_F], [b_F])
        self.tt("dve", Fh, Fh, oml, ALU.mult, [b_F, b_lb], [b_F])
        self.tt("dve", Fh, Fh, lb, ALU.add, [b_F, b_lb], [b_F])
        self.ts("pool", KKh, Fh, -1.0, 1.0, ALU.mult, ALU.add, [b_F], [b_KK])
        self.ts("dve", Fh, Fh, 1e-30, None, ALU.max, None, [b_F, b_KK], [b_F])
        self.act(Fh, Fh, AF.Ln, [b_F], [b_F])
        yield
        self.mm(hb5, U16, Fh, [bc, b_F], B5)
        self.mm(hb6, O16, Fh, [bc, b_F], B6)
        qth, b_qt = qsl[:, b, :], b_qsl[b]
        kth, b_kt = KKh, b_KK
        self.act(EXh, hb6, AF.Exp, B6, [b_EX])
        self.tt("pool", kdh, KKh, EXh, ALU.mult, [b_KK, b_EX], [b_kd])
        self.act(EXh, hb5, AF.Exp, B5 + [b_EX], [b_EX])
        self.tt("dve", qth, qth, EXh, ALU.mult, [b_qt, b_EX], [b_qt])
        self.act(EXh, hb5, AF.Exp, B5 + [b_EX], [b_EX], scale=-1.0)
        self.tt("dve", kth, KKh, EXh, ALU.mult, [b_KK, b_EX, b_kd], [b_kt])
        yield
        for h in range(4):
            self.tr(hc(hb5, h), hc(qth, h), idf, [b_qt, bc], [B5[h]])
            self.tr(hc(hb6, h), hc(kth, h), idf, [b_kt, bc], [B6[h]])
        self.act(qtTh, hb5, AF.Identity, B5, [b_qtT])
        self.cp("dve", ktTh, hb6, B6, [b_ktT])
        yield
        for h in range(4):
            self.mm(hc(hb5, h), hc(ktTh, h), hc(qtTh, h), [b_ktT, b_qtT], [B5[h]])
        for h in range(4):
            self.tt("dve", hc(scTh, h), hc(hb5, h), U16, ALU.mult, [B5[h], bc], [b_scT[h]])
        yield
        for h in range(4):
            self.mm(hb6[:, h * 8:(h + 1) * 8], hc(Fh, h), IND8, [b_F, bc], [B6[0]])
        self.act(dSh, hb6[:, 0:32], AF.Exp, [B6[0]], [b_dS])
        yield
        for h in range(4):
            self.mm(hc(hb5, h), hc(vbf[:, b, :], h), hc(scTh, h), [b_vbf[b], b_scT[h]], [B5[h]],
                    start=(h == 0), stop=False, skip_group_check=True)
        for c in range(8):
            cur = shb_cur["i"]
            nxt = 1 - cur
            vm, bvm = vmh[c % 2], b_vm[c % 2]
            self.ts("pool", vm, vbf[:, b, :], IND8[:, c:c + 1], 1.0, ALU.mult, ALU.mult, [b_vbf[b], bc], [bvm])
            for h in range(4):
                self.mm(hc(hb5, h)[:, c * 16:(c + 1) * 16], hc(Shb[cur], h), hc(qtTh, h)[:, c * 16:(c + 1) * 16],
                        [b_Shb[cur], b_qtT], [B5[h]], start=False, stop=(c == 7), skip_group_check=True)
            for h in range(4):
                self.mm(hc(hb6, h), hc(kdh, h), hc(vm, h), [b_kd, bvm], [B6[h]])
            for h in range(4):
                self.stt(hc(Sh, h), hc(Sh, h), dSh[:, h * 8 + c:h * 8 + c + 1], hc(hb6, h), ALU.mult, ALU.add,
                         [B6[h], b_Sh[h], b_dS], [b_Sh[h]])
            self.act(Shb[nxt], Sh, AF.Identity, b_Sh, [b_Shb[nxt]])
            shb_cur["i"] = nxt
            yield
        yield
        for h in range(4):
            self.act(hc(kdh, h), hc(hb5, h), AF.Square, [B5[h], b_kd], [b_kd])
        for h in range(4):
            self.mm(hc(hb6, h), self.ones_bf, hc(kdh, h), [bc, b_kd], [B6[h]])
        yield
        for h in range(4):
            self.act(hc(EXh, h), hc(hb6, h), AF.Ln, [B6[h], bc, b_EX], [b_EX], scale=1.0 / 128, bias=self.eps_ap)
            self.act(hc(EXh, h), hc(EXh, h), AF.Exp, [b_EX], [b_EX], scale=-0.5)
            self.stt(hc(Fh, h), hc(hb5, h), hnorm, hc(EXh, h), ALU.mult, ALU.mult, [B5[h], b_EX, b_F, bc], [b_F])
            self.tt("pool", oTq[:, 4 + h, bsl], hc(Fh, h), gbg[:, h, bsl], ALU.mult, [b_F, b_gbg[h]], [b_oq[4 + h]])


FULL_PLAN = [("even", 0), ("ffn", 0), ("odd", 1), ("ffn", 1), ("even", 2), ("ffn", 2), ("odd", 3), ("ffn", 3)]
_CACHE = {}


def prep_shared(inp):
    f = lambda k: np.asarray(inp[k], np.float32)
    sh = {}
    sh["consts"] = make_consts()
    g = np.concatenate([f("norm_mix"), f("norm_ffn"), f("norm_final")[None, :]], axis=0)
    sh["gains"] = pvec(g).reshape(128, 9 * NCH)
    cw = np.concatenate([f("ffn_conv_w"), f("ffn_conv_b")[:, None, :]], axis=1)
    sh["ffn_cw"] = np.ascontiguousarray(pvec(cw).transpose(0, 1, 3, 2)).reshape(128, DEPTH * NFC * 4)
    wup = f("ffn_w_up")
    sh["ffn_wup"] = np.stack([chunk_cols(wup[l], 2 * NFC) for l in range(DEPTH)]).reshape(DEPTH, 2 * NFC, 128, NCH * 128)
    sh["ffn_wdn"] = np.ascontiguousarray(f("ffn_w_down").reshape(DEPTH, NFC, 128, D))
    prep_mixers(inp, sh)
    return sh


def prep_mixers(inp, sh):
    f = lambda k: np.asarray(inp[k], np.float32)
    cwin = f("c_w_in")
    sh["c_win"] = np.stack([chunk_cols(cwin[j], 16) for j in range(2)]).reshape(2, 16, 128, NCH * 128)
    sh["c_gw"] = np.ascontiguousarray(np.stack([f("c_gate_a_w"), f("c_gate_x_w")], axis=1))
    sh["c_wout"] = np.ascontiguousarray(f("c_w_out").reshape(2, NCH, 128, D))
    pvs = np.concatenate([f("c_conv_w"), f("c_conv_b")[:, None], f("c_gate_a_b")[:, None], f("c_gate_x_b")[:, None],
                          f("c_lambda")[:, None]], axis=1)
    sh["c_pv"] = np.ascontiguousarray(pvec(pvs).transpose(0, 1, 3, 2)).reshape(128, 2 * NCH * 8)
    prep_even(inp, sh)


def prep_even(inp, sh):
    f = lambda k: np.asarray(inp[k], np.float32)
    w = f("ab_w_in")
    fm = np.concatenate([w[:, :, 0:2048], w[:, :, 3592:4104]], axis=2)
    sh["ab_fm"] = np.stack([chunk_cols(fm[j], 20) for j in range(2)]).reshape(2, 20, 128, NCH * 128)
    tm = np.concatenate([w[:, :, 2056:3592], w[:, :, 2048:2056]], axis=2)
    sh["ab_tm"] = np.ascontiguousarray(tm.reshape(2, NCH, 128, 1544).transpose(0, 2, 1, 3)).reshape(2, 128, NCH * 1544)
    sh["ab_wout"] = np.ascontiguousarray(f("ab_w_out").reshape(2, NCH, 128, D))
    cw = np.ascontiguousarray(pvec(f("gdn_conv_w")).transpose(0, 1, 3, 2)).reshape(128, 2, 48)
    gn = f("gdn_norm").T[:, :, None]
    hn = f("hgrn_norm").T[:, :, None]
    sh["e_pv"] = np.ascontiguousarray(np.concatenate([cw, gn, hn], axis=2)).reshape(128, 100)
    sh["e_bc"] = np.ascontiguousarray(np.concatenate([f("gdn_a_log").reshape(-1), f("gdn_dt_bias").reshape(-1),
                                                      f("hgrn_lower_bounds").reshape(-1)]))


def run(inp, plan, n_cores=8, splits=None):
    sh = prep_shared(inp)
    x = np.asarray(inp["x"], np.float32)
    xts = [np.ascontiguousarray(x[b].T).reshape(NCH, 128, T) for b in range(n_cores)]
    parts = splits if splits is not None else [plan]
    for pi, part in enumerate(parts):
        last = pi == len(parts) - 1
        key = (tuple(part), last)
        if key not in _CACHE:
            _CACHE[key] = Builder(list(part), do_final_norm=last).build()
        nc = _CACHE[key]
        in_maps = []
        for b in range(n_cores):
            m = dict(sh)
            m["xT"] = xts[b]
            in_maps.append(m)
        res = run_bass_kernel_spmd(nc, in_maps, core_ids=list(range(n_cores)))
        xts = [np.ascontiguousarray(r["yT"]) for r in res.results]
    out = np.stack([np.ascontiguousarray(r.reshape(D, T).T) for r in xts])
    return out


N_SPLIT = 1


def kernel(**inputs):
    if N_SPLIT == 1:
        return run(inputs, FULL_PLAN, 8).astype(np.float32)
    return run(inputs, FULL_PLAN, 8, splits=[FULL_PLAN[:4], FULL_PLAN[4:]]).astype(np.float32)
```

```python
import contextlib
import numpy as np
import concourse.bass as bass
import concourse.mybir as mybir
from concourse.bass_utils import run_bass_kernel_spmd

F32 = mybir.dt.float32
BF16 = mybir.dt.bfloat16
AF = mybir.ActivationFunctionType
ALU = mybir.AluOpType

D = 1024
T = 2048
NCH = 8
DFF = 2816
NFC = 22
DEPTH = 4
EPS = 1e-6
ENGS = ("pe", "act", "dve", "pool", "sp")


class Buf:
    __slots__ = ("name", "w", "r", "excl")

    def __init__(self, name, excl=False):
        self.name = name
        self.w = None
        self.r = []
        self.excl = excl


class Instr:
    __slots__ = ("eng", "fn", "deps", "signal", "dma", "sem", "val")

    def __init__(self, eng, fn, dma):
        self.eng = eng
        self.fn = fn
        self.deps = []
        self.signal = False
        self.dma = dma
        self.sem = None
        self.val = None


class Prog:
    def __init__(self, nc, same_engine_sync=True):
        self.nc = nc
        self.streams = {e: [] for e in ENGS}
        self.same_engine_sync = same_engine_sync
        self.dma_slots = {}
        self.fence_deps = {e: [] for e in ENGS}
        self.count = 0

    def _add_dep(self, ins, d):
        if d is ins:
            return
        if d.dma is not None:
            d = self.dma_slots[d.dma][-1]
            if d is ins:
                return
        elif ins.dma is None and d.eng == ins.eng:
            if ins.eng == "pe" or not self.same_engine_sync:
                return
        ins.deps.append(d)
        d.signal = True

    def op(self, eng, fn, reads=(), writes=(), dma=None):
        ins = Instr(eng, fn, dma)
        self.count += 1
        deps = {}
        for b in reads:
            if b.w is not None:
                deps[id(b.w)] = b.w
            if b.excl:
                for r in b.r:
                    if r.eng != eng:
                        deps[id(r)] = r
        for b in writes:
            if b.w is not None:
                deps[id(b.w)] = b.w
            for r in b.r:
                deps[id(r)] = r
        for d in self.fence_deps[eng]:
            deps[id(d)] = d
        self.fence_deps[eng] = []
        for d in deps.values():
            self._add_dep(ins, d)
        key = eng if dma is None else ("dma", dma)
        for b in reads:
            b.r = [r for r in b.r if (r.eng if r.dma is None else ("dma", r.dma)) != key]
            b.r.append(ins)
        for b in writes:
            b.w = ins
            b.r = []
        self.streams[eng].append(ins)
        if dma is not None:
            self.dma_slots.setdefault(dma, []).append(ins)
        return ins

    def fence(self):
        last = []
        for e in ENGS:
            for ins in reversed(self.streams[e]):
                if ins.dma is None:
                    last.append(ins)
                    break
        for lst in self.dma_slots.values():
            last.append(lst[-1])
        for e in ENGS:
            self.fence_deps[e] = list(last)

    def emit(self, final_waits=()):
        nc = self.nc
        for f in final_waits:
            f.signal = True
        with contextlib.ExitStack() as es:
            eng_sem = {e: es.enter_context(nc.semaphore("s_" + e)) for e in ENGS}
            slot_sem = {s: es.enter_context(nc.semaphore("d_" + str(s))) for s in self.dma_slots}
            for e in ENGS:
                c = 0
                for ins in self.streams[e]:
                    if ins.dma is None and ins.signal:
                        c += 1
                        ins.sem = eng_sem[e]
                        ins.val = c
            for s, lst in self.dma_slots.items():
                c = 0
                for ins in lst:
                    c += 16
                    ins.sem = slot_sem[s]
                    ins.val = c
                    ins.signal = True
            block = es.enter_context(nc.Block())

            def run_stream(e, engobj):
                waited = {}
                for ins in self.streams[e]:
                    need = {}
                    for d in ins.deps:
                        k = id(d.sem)
                        if waited.get(k, 0) >= d.val:
                            continue
                        if k not in need or need[k][1] < d.val:
                            need[k] = (d.sem, d.val)
                    for k, (sem, val) in need.items():
                        engobj.wait_ge(sem, val)
                        waited[k] = val
                    bi = ins.fn(engobj)
                    if ins.signal:
                        bi.then_inc(ins.sem, 16 if ins.dma is not None else 1)
                if e == "sp":
                    fw = {}
                    for f in final_waits:
                        f = self.dma_slots[f.dma][-1]
                        fw[id(f.sem)] = f
                    for f in fw.values():
                        engobj.wait_ge(f.sem, f.val)

            @block.tensor
            def _(eng):
                run_stream("pe", eng)

            @block.scalar
            def _(eng):
                run_stream("act", eng)

            @block.vector
            def _(eng):
                run_stream("dve", eng)

            @block.gpsimd
            def _(eng):
                run_stream("pool", eng)

            @block.sync
            def _(eng):
                run_stream("sp", eng)


class Arena:
    def __init__(self, handle, nbytes):
        self.h = handle
        self.n = nbytes
        self.off = 0

    def mark(self):
        return self.off

    def release(self, m):
        self.off = m

    def alloc(self, shape, dt):
        esz = 4 if dt == F32 else 2
        nel = 1
        for s in shape[1:]:
            nel *= s
        nb = nel * esz
        self.off = (self.off + 31) // 32 * 32
        assert self.off + nb <= self.n, ("arena overflow", self.off, nb, self.n)
        a = self.h[:, self.off // 2:(self.off + nb) // 2]
        self.off += nb
        if dt == F32:
            a = a.bitcast(F32)
        if len(shape) == 3:
            a = a.rearrange("p (a b) -> p a b", a=shape[1])
        elif len(shape) == 4:
            a = a.rearrange("p (a b c) -> p a b c", a=shape[1], b=shape[2])
        return a


C_IDENT, C_ONES, C_U128, C_SL128, C_MSL, C_U16, C_O16, C_IND8 = 0, 128, 256, 384, 512, 640, 768, 896
NCONST = 904


def make_consts():
    c = np.zeros((128, NCONST), np.float32)
    i = np.arange(128)
    c[:, C_IDENT:C_IDENT + 128] = np.eye(128)
    c[:, C_ONES:C_ONES + 128] = 1.0
    c[:, C_U128:C_U128 + 128] = (i[:, None] <= i[None, :])
    c[:, C_SL128:C_SL128 + 128] = (i[:, None] > i[None, :])
    c[:, C_MSL:C_MSL + 128] = (i[:, None] > i[None, :])
    same = (i[:, None] // 16) == (i[None, :] // 16)
    c[:, C_U16:C_U16 + 128] = same & (i[:, None] <= i[None, :])
    c[:, C_O16:C_O16 + 128] = same & (i[:, None] > i[None, :])
    c[:, C_IND8:C_IND8 + 8] = (i[:, None] // 16) == np.arange(8)[None, :]
    return c


def chunk_cols(w, ncols_chunks):
    k, n = w.shape
    assert k == D and n == ncols_chunks * 128
    return np.ascontiguousarray(w.reshape(NCH, 128, ncols_chunks, 128).transpose(2, 1, 0, 3))


def pvec(v):
    v = np.asarray(v, np.float32)
    n = v.shape[-1] // 128
    v = v.reshape(v.shape[:-1] + (n, 128))
    return np.ascontiguousarray(np.moveaxis(v, -1, 0))


class Builder:
    def __init__(self, plan, do_final_norm=True):
        self.do_final_norm = do_final_norm
        self.NB = 2
        self.plan = plan
        self.nc = bass.Bass("TRN2", target_bir_lowering=False)
        self.P = Prog(self.nc)
        self.dram = {}

    def din(self, name, shape, dt=F32):
        t = self.nc.dram_tensor(name, list(shape), dt, kind="ExternalInput").ap()
        self.dram[name] = t
        return t

    def bank(self, b, n=512, off=0):
        return self.psum[:, b * 512 + off: b * 512 + off + n]

    def cst(self, off, n=128):
        return self.consts[:, off:off + n]

    def build(self):
        nc, P = self.nc, self.P
        with contextlib.ExitStack() as es:
            self.es = es
            d = self.din
            xT_d = d("xT", [NCH, 128, T])
            consts_d = d("consts", [128, NCONST])
            gains_d = d("gains", [128, 9 * NCH])
            fcw_d = d("ffn_cw", [128, DEPTH * NFC * 4])
            self.wup_d = d("ffn_wup", [DEPTH, 2 * NFC, 128, NCH * 128])
            self.wdn_d = d("ffn_wdn", [DEPTH, NFC, 128, D])
            self.declare_mixer_inputs()
            self.out_d = nc.dram_tensor("yT", [NCH, 128, T], F32, kind="ExternalOutput").ap()

            sb = lambda name, shape, dt: es.enter_context(nc.sbuf_tensor(name, shape, dt))
            self.xT = sb("xTs", [128, NCH, T], F32)
            self.consts = sb("consts_s", [128, NCONST], F32)
            self.cbf = sb("cbf", [128, 256], BF16)
            self.gains = sb("gains_s", [128, 9 * NCH], F32)
            self.fcw = sb("fcw_s", [128, DEPTH * NFC * 4], F32)
            self.eps_t = sb("eps_t", [128, 2], F32)
            ARENA_BYTES = 136 * 1024
            arena_h = sb("arena", [128, ARENA_BYTES // 2], BF16)
            self.arena = Arena(arena_h, ARENA_BYTES)
            self.psum = es.enter_context(nc.psum_tensor("psum", [128, 4096], F32))

            self.b_x = [[Buf("x%d_%d" % (c, tb)) for tb in range(16)] for c in range(NCH)]
            self.b_bk = [Buf("bank%d" % b, excl=True) for b in range(8)]
            self.b_const = Buf("const")
            self.bank_rr = 0

            xT = self.xT
            for c in range(NCH):
                P.op("sp", lambda e, c=c: e.dma_start(out=xT[:, c, :], in_=xT_d[c]),
                     writes=self.b_x[c], dma="x%d" % (c % 4))
            P.op("sp", lambda e: e.dma_start(out=self.consts[:], in_=consts_d), writes=[self.b_const], dma="c0")
            P.op("sp", lambda e: e.dma_start(out=self.gains[:], in_=gains_d), writes=[self.b_const], dma="c0")
            P.op("sp", lambda e: e.dma_start(out=self.fcw[:], in_=fcw_d), writes=[self.b_const], dma="c0")
            self.load_mixer_consts()
            P.op("dve", lambda e: e.tensor_copy(out=self.cbf[:, 0:256], in_=self.consts[:, 0:256]),
                 reads=[self.b_const], writes=[self.b_const])
            P.op("dve", lambda e: e.memset(self.eps_t[:, 0:1], EPS), writes=[self.b_const])
            P.op("dve", lambda e: e.memset(self.eps_t[:, 1:2], 1.0), writes=[self.b_const])
            self.eps_ap = self.eps_t[:, 0:1]
            self.one_ap = self.eps_t[:, 1:2]
            self.ident_bf = self.cbf[:, 0:128]
            self.ones_bf = self.cbf[:, 128:256]
            self.ident_f = self.cst(C_IDENT)
            self.ones_f = self.cst(C_ONES)
            P.fence()

            for kind, l in self.plan:
                if kind == "ffn":
                    self.ffn(l)
                elif kind == "odd":
                    self.odd_mixer(l)
                elif kind == "even":
                    self.even_mixer(l)
                P.fence()
            self.final_norm()
            P.emit(final_waits=self.final_dmas)
        return nc

    def bx(self, c, t0, n):
        return self.b_x[c][t0 // 128:(t0 + n) // 128]

    def bb(self, b):
        return [self.b_bk[b]]

    def next_bank(self):
        b = self.bank_rr
        self.bank_rr = (b + 1) % 8
        return b

    def norm_block(self, tb, gcol, dst, b_dst, sq, b_sq, rs, b_rs):
        P, xT = self.P, self.xT
        ts = slice(tb * 512, (tb + 1) * 512)
        for c in range(NCH):
            P.op("act", lambda e, c=c: e.activation(out=sq[:, c, :], in_=xT[:, c, ts], func=AF.Square),
                 reads=[*self.bx(c, tb * 512, 512)], writes=[b_sq[c]])
        bi = self.next_bank()
        bk = self.bank(bi)
        for c in range(NCH):
            P.op("pe", lambda e, c=c: e.matmul(bk, lhsT=self.ones_bf, rhs=sq[:, c, :], start=(c == 0), stop=(c == NCH - 1)),
                 reads=[b_sq[c], self.b_const], writes=self.bb(bi))
        P.op("act", lambda e: e.activation(out=rs, in_=bk, func=AF.Sqrt, scale=1.0 / D, bias=self.eps_ap),
             reads=self.bb(bi) + [self.b_const], writes=[b_rs])
        P.op("dve", lambda e: e.reciprocal(out=rs, in_=rs), reads=[b_rs], writes=[b_rs])
        for c in range(NCH):
            g = self.gains[:, gcol + c: gcol + c + 1]
            P.op("dve", lambda e, c=c, g=g: e.scalar_tensor_tensor(out=dst(c), in0=xT[:, c, ts], scalar=g, in1=rs,
                                                                  op0=ALU.mult, op1=ALU.mult),
                 reads=[*self.bx(c, tb * 512, 512), b_rs, self.b_const], writes=b_dst(c))

    def ffn(self, l):
        P, A, xT = self.P, self.arena, self.xT
        m0 = A.mark()
        hT = A.alloc([128, NCH, T], BF16)
        b_h = [[Buf("h%d_%d" % (c, tb)) for tb in range(4)] for c in range(NCH)]
        wup = [A.alloc([128, 2, NCH * 128], BF16) for _ in range(3)]
        b_wup = [Buf("wup%d" % s) for s in range(3)]
        G = 11
        wdn = A.alloc([128, G, D], BF16)
        b_wdn = [Buf("wdn%d" % i) for i in range(G)]
        m1 = A.mark()

        jstate = {"n": 0}

        def load_wup(j):
            s = jstate["n"] % 3
            jstate["n"] += 1
            for gv in range(2):
                src = self.wup_d[l, gv * NFC + j]
                P.op("pool", lambda e, s=s, gv=gv, src=src: e.dma_start(out=wup[s][:, gv, :], in_=src),
                     writes=[b_wup[s]], dma="wup%d" % s)
            return s

        def load_wdn(g):
            for i in range(G):
                src = self.wdn_d[l, g * G + i]
                P.op("pool", lambda e, i=i, src=src: e.dma_start(out=wdn[:, i, :], in_=src),
                     writes=[b_wdn[i]], dma="wdn%d" % (i % 2))

        slots = {}
        slots[0] = load_wup(0)
        slots[1] = load_wup(1)
        load_wdn(0)

        sq = [A.alloc([128, NCH, 512], BF16) for _ in range(2)]
        rs = [A.alloc([128, 512], F32) for _ in range(2)]
        b_sq = [[Buf("sq") for c in range(NCH)] for _ in range(2)]
        b_rs = [Buf("rs") for _ in range(2)]
        for tb in range(4):
            ts = slice(tb * 512, (tb + 1) * 512)
            self.norm_block(tb, (4 + l) * NCH, lambda c, ts=ts: hT[:, c, ts], lambda c, tb=tb: [b_h[c][tb]],
                            sq[tb % 2], b_sq[tb % 2], rs[tb % 2], b_rs[tb % 2])
        A.release(m1)
        P.fence()

        act = A.alloc([128, G, T], BF16)
        b_act = [[Buf("act") for h in range(2)] for i in range(G)]
        ctmp = [A.alloc([128, 1024], F32) for _ in range(2)]
        stmp = [A.alloc([128, 1024], BF16) for _ in range(2)]
        b_ct = [Buf("ct") for _ in range(2)]
        b_st = [Buf("st") for _ in range(2)]
        halo = A.alloc([128, 2], F32)
        b_halo = Buf("halo")
        cw = lambda j, k: self.fcw[:, (l * NFC + j) * 4 + k: (l * NFC + j) * 4 + k + 1]

        step = 0
        for g in range(2):
            for jj in range(G):
                j = g * G + jj
                if j + 2 < NFC:
                    slots[j + 2] = load_wup(j + 2)
                s = slots[j]
                for half in range(2):
                    ps = step % 2
                    step += 1
                    gb = [4 * ps, 4 * ps + 1]
                    vb = [4 * ps + 2, 4 * ps + 3]
                    Gp = self.psum[:, ps * 2048: ps * 2048 + 1024]
                    Vp = self.psum[:, ps * 2048 + 1024: ps * 2048 + 2048]
                    for gv, banks in ((0, gb), (1, vb)):
                        for t2 in range(2):
                            tb = half * 2 + t2
                            out = self.bank(banks[t2])
                            for k in range(NCH):
                                P.op("pe", lambda e, out=out, s=s, gv=gv, k=k, tb=tb: e.matmul(
                                    out, lhsT=wup[s][:, gv, k * 128:(k + 1) * 128], rhs=hT[:, k, tb * 512:(tb + 1) * 512],
                                    start=(k == 0), stop=(k == NCH - 1)),
                                    reads=[b_wup[s], b_h[k][tb]], writes=self.bb(banks[t2]))
                    ct, st = ctmp[ps], stmp[ps]
                    bG = self.bb(gb[0]) + self.bb(gb[1])
                    bV = self.bb(vb[0]) + self.bb(vb[1])
                    P.op("act", lambda e, ct=ct, Gp=Gp, j=j: e.activation(out=ct, in_=Gp, func=AF.Identity,
                                                                          scale=cw(j, 2), bias=cw(j, 3)),
                         reads=bG + [self.b_const], writes=[b_ct[ps]])
                    P.op("dve", lambda e, ct=ct, Gp=Gp, j=j: e.scalar_tensor_tensor(
                        out=ct[:, 1:1024], in0=Gp[:, 0:1023], scalar=cw(j, 1), in1=ct[:, 1:1024], op0=ALU.mult, op1=ALU.add),
                        reads=bG + [b_ct[ps]], writes=[b_ct[ps]])
                    P.op("dve", lambda e, ct=ct, Gp=Gp, j=j: e.scalar_tensor_tensor(
                        out=ct[:, 2:1024], in0=Gp[:, 0:1022], scalar=cw(j, 0), in1=ct[:, 2:1024], op0=ALU.mult, op1=ALU.add),
                        reads=bG + [b_ct[ps]], writes=[b_ct[ps]])
                    if half == 1:
                        P.op("dve", lambda e, ct=ct, j=j: e.scalar_tensor_tensor(
                            out=ct[:, 0:1], in0=halo[:, 1:2], scalar=cw(j, 1), in1=ct[:, 0:1], op0=ALU.mult, op1=ALU.add),
                            reads=[b_halo, b_ct[ps]], writes=[b_ct[ps]])
                        P.op("dve", lambda e, ct=ct, j=j: e.scalar_tensor_tensor(
                            out=ct[:, 0:2], in0=halo[:, 0:2], scalar=cw(j, 0), in1=ct[:, 0:2], op0=ALU.mult, op1=ALU.add),
                            reads=[b_halo, b_ct[ps]], writes=[b_ct[ps]])
                    else:
                        P.op("act", lambda e, Gp=Gp: e.activation(out=halo, in_=Gp[:, 1022:1024], func=AF.Identity),
                             reads=bG, writes=[b_halo])
                    P.op("act", lambda e, ct=ct, st=st: e.activation(out=st, in_=ct, func=AF.Silu),
                         reads=[b_ct[ps]], writes=[b_st[ps]])
                    P.op("dve", lambda e, st=st, Vp=Vp, jj=jj, half=half: e.tensor_tensor(
                        out=act[:, jj, half * 1024:(half + 1) * 1024], in0=Vp, in1=st, op=ALU.mult),
                        reads=bV + [b_st[ps]], writes=[b_act[jj][half]])
            for dch in range(NCH):
                for tb in range(4):
                    bi = self.next_bank()
                    bk = self.bank(bi)
                    for i in range(G):
                        P.op("pe", lambda e, bk=bk, i=i, dch=dch, tb=tb: e.matmul(
                            bk, lhsT=wdn[:, i, dch * 128:(dch + 1) * 128], rhs=act[:, i, tb * 512:(tb + 1) * 512],
                            start=(i == 0), stop=(i == G - 1)),
                            reads=[b_wdn[i], b_act[i][tb // 2]], writes=self.bb(bi))
                    P.op("dve", lambda e, bk=bk, dch=dch, tb=tb: e.tensor_tensor(
                        out=xT[:, dch, tb * 512:(tb + 1) * 512], in0=bk, in1=xT[:, dch, tb * 512:(tb + 1) * 512], op=ALU.add),
                        reads=self.bb(bi) + self.bx(dch, tb * 512, 512), writes=self.bx(dch, tb * 512, 512))
            if g == 0:
                load_wdn(1)
        A.release(m0)

    def final_norm(self):
        P, A, xT = self.P, self.arena, self.xT
        m0 = A.mark()
        sq = [A.alloc([128, NCH, 512], BF16) for _ in range(2)]
        rs = [A.alloc([128, 512], F32) for _ in range(2)]
        b_sq = [[Buf("sq") for c in range(NCH)] for _ in range(2)]
        b_rs = [Buf("rs") for _ in range(2)]
        for tb in range(4 if self.do_final_norm else 0):
            ts = slice(tb * 512, (tb + 1) * 512)
            self.norm_block(tb, 8 * NCH, lambda c, ts=ts: xT[:, c, ts], lambda c, tb=tb: self.bx(c, tb * 512, 512),
                            sq[tb % 2], b_sq[tb % 2], rs[tb % 2], b_rs[tb % 2])
        self.final_dmas = []
        for c in range(NCH):
            ins = P.op("sp", lambda e, c=c: e.dma_start(out=self.out_d[c], in_=xT[:, c, :]),
                       reads=self.b_x[c], dma="out%d" % (c % 4))
            self.final_dmas.append(ins)
        A.release(m0)

    def declare_mixer_inputs(self):
        d = self.din
        self.cwin_d = d("c_win", [2, 16, 128, NCH * 128])
        self.cgw_d = d("c_gw", [2, 2, 4, 256, 256])
        self.cwout_d = d("c_wout", [2, NCH, 128, D])
        self.cpv_d = d("c_pv", [128, 2 * NCH * 8])
        self.declare_even_inputs()

    def load_mixer_consts(self):
        P = self.P
        self.cpv = self.es.enter_context(self.nc.sbuf_tensor("cpv_s", [128, 2 * NCH * 8], F32))
        P.op("sp", lambda e: e.dma_start(out=self.cpv[:], in_=self.cpv_d), writes=[self.b_const], dma="c0")
        self.load_even_consts()

    def odd_mixer(self, l):
        P, A, xT = self.P, self.arena, self.xT
        j = l // 2
        m0 = A.mark()
        win = A.alloc([128, 16, NCH * 128], BF16)
        b_win = [Buf("win") for _ in range(16)]
        gw = A.alloc([128, 16, 256], BF16)
        b_gw = Buf("gw")
        wout = A.alloc([128, NCH, D], BF16)
        b_wout = [Buf("wout") for _ in range(NCH)]
        for c in range(16):
            P.op("pool", lambda e, c=c: e.dma_start(out=win[:, c, :], in_=self.cwin_d[j, c]),
                 writes=[b_win[c]], dma="mw%d" % (c % 2))
        for ax in range(2):
            for hd in range(4):
                src = self.cgw_d[j, ax, hd].rearrange("(kc p) n -> p kc n", p=128)
                i0 = (ax * 4 + hd) * 2
                P.op("pool", lambda e, i0=i0, src=src: e.dma_start(out=gw[:, i0:i0 + 2, :], in_=src),
                     writes=[b_gw], dma="mw2")
        for k in range(NCH):
            P.op("pool", lambda e, k=k: e.dma_start(out=wout[:, k, :], in_=self.cwout_d[j, k]),
                 writes=[b_wout[k]], dma="mw%d" % (k % 2))

        pv = lambda c, k: self.cpv[:, (j * NCH + c) * 8 + k:(j * NCH + c) * 8 + k + 1]
        pvall = self.cpv[:, j * NCH * 8:(j + 1) * NCH * 8].rearrange("p (c k) -> p c k", k=8)
        der = A.alloc([128, 8, NCH], F32)
        b_der = Buf("der")
        lam = pvall[:, :, 7]
        R_, W_ = [self.b_const, b_der], [b_der]
        P.op("act", lambda e: e.activation(out=der[:, 0, :], in_=lam, func=AF.Exp, scale=-1.0), reads=R_, writes=W_)
        P.op("dve", lambda e: e.tensor_scalar(out=der[:, 1, :], in0=der[:, 0, :], scalar1=1.0, scalar2=None, op0=ALU.add), reads=R_, writes=W_)
        P.op("act", lambda e: e.activation(out=der[:, 2, :], in_=der[:, 1, :], func=AF.Ln), reads=R_, writes=W_)
        P.op("dve", lambda e: e.tensor_scalar(out=der[:, 3, :], in0=der[:, 1, :], scalar1=-1.0, scalar2=1e-30, op0=ALU.add, op1=ALU.max), reads=R_, writes=W_)
        P.op("dve", lambda e: e.reciprocal(out=der[:, 3, :], in_=der[:, 3, :]), reads=R_, writes=W_)
        P.op("dve", lambda e: e.tensor_tensor(out=der[:, 2, :], in0=der[:, 2, :], in1=der[:, 0, :], op=ALU.mult), reads=R_, writes=W_)
        P.op("dve", lambda e: e.tensor_tensor(out=der[:, 2, :], in0=der[:, 2, :], in1=der[:, 3, :], op=ALU.mult), reads=R_, writes=W_)
        P.op("dve", lambda e: e.tensor_scalar(out=der[:, 4, :], in0=der[:, 2, :], scalar1=-4.0, scalar2=None, op0=ALU.mult), reads=R_, writes=W_)
        P.op("dve", lambda e: e.tensor_scalar(out=der[:, 5, :], in0=der[:, 2, :], scalar1=-8.0, scalar2=None, op0=ALU.mult), reads=R_, writes=W_)
        P.op("dve", lambda e: e.tensor_scalar(out=der[:, 6, :], in0=pvall[:, :, 5], scalar1=0.5, scalar2=None, op0=ALU.mult), reads=R_, writes=W_)
        P.op("dve", lambda e: e.tensor_scalar(out=der[:, 7, :], in0=pvall[:, :, 6], scalar1=0.5, scalar2=None, op0=ALU.mult), reads=R_, writes=W_)
        dv = lambda k, c: der[:, k, c:c + 1]

        hTq = A.alloc([128, NCH, 512], BF16)
        b_hq = [Buf("hq") for _ in range(NCH)]
        sq = A.alloc([128, NCH, 512], BF16)
        b_sq = [Buf("sq") for _ in range(NCH)]
        rs = A.alloc([128, 512], F32)
        b_rs = Buf("rs")
        gate = A.alloc([128, NCH, 512], BF16)
        b_gate = [Buf("gate") for _ in range(NCH)]
        obf = A.alloc([128, NCH, 512], BF16)
        b_obf = [Buf("obf") for _ in range(NCH)]
        xc = [A.alloc([128, 2, 512], F32) for _ in range(2)]
        xcb = [A.alloc([128, 2, 512], BF16) for _ in range(2)]
        b_xc = [[Buf("xc") for _ in range(2)] for _ in range(2)]
        b_xcb = [[Buf("xcb") for _ in range(2)] for _ in range(2)]
        halo = [A.alloc([128, NCH, 4], F32) for _ in range(2)]
        b_halo = [[Buf("halo") for _ in range(NCH)] for _ in range(2)]
        carry = [A.alloc([128, NCH], F32) for _ in range(2)]
        b_carry = [[Buf("carry") for _ in range(NCH)] for _ in range(2)]
        NT = 5
        tmp = [[A.alloc([128, 512], F32) for _ in range(NT)] for _ in range(2)]
        b_tmp = [[Buf("tmp") for _ in range(NT)] for _ in range(2)]
        it = 0
        for q in range(4):
            self.norm_block(q, l * NCH, lambda c: hTq[:, c, :], lambda c: [b_hq[c]], sq, b_sq, rs, b_rs)
            for c in range(NCH):
                bi = self.next_bank()
                bk = self.bank(bi)
                for k in range(NCH):
                    P.op("pe", lambda e, bk=bk, c=c, k=k: e.matmul(bk, lhsT=win[:, c, k * 128:(k + 1) * 128], rhs=hTq[:, k, :],
                                                                 start=(k == 0), stop=(k == NCH - 1)),
                         reads=[b_win[c], b_hq[k]], writes=self.bb(bi))
                P.op("act", lambda e, bk=bk, c=c: e.activation(out=gate[:, c, :], in_=bk, func=AF.Gelu_apprx_tanh),
                     reads=self.bb(bi), writes=[b_gate[c]])
            hcur, hprev = halo[q % 2], halo[(q + 1) % 2]
            bhcur, bhprev = b_halo[q % 2], b_halo[(q + 1) % 2]
            ccur, cprev = carry[q % 2], carry[(q + 1) % 2]
            bccur, bcprev = b_carry[q % 2], b_carry[(q + 1) % 2]
            for hd in range(4):
                xs = hd % 2
                for kc in range(2):
                    c = 2 * hd + kc
                    bi = self.next_bank()
                    bk = self.bank(bi)
                    for k in range(NCH):
                        P.op("pe", lambda e, bk=bk, c=c, k=k: e.matmul(bk, lhsT=win[:, 8 + c, k * 128:(k + 1) * 128], rhs=hTq[:, k, :],
                                                                     start=(k == 0), stop=(k == NCH - 1)),
                             reads=[b_win[8 + c], b_hq[k]], writes=self.bb(bi))
                    xcc = xc[xs][:, kc, :]
                    bx = b_xc[xs][kc]
                    P.op("act", lambda e, bk=bk, xcc=xcc, c=c: e.activation(out=xcc, in_=bk, func=AF.Identity, scale=pv(c, 3), bias=pv(c, 4)),
                         reads=self.bb(bi) + [self.b_const], writes=[bx])
                    P.op("act", lambda e, bk=bk, c=c, hcur=hcur: e.activation(out=hcur[:, c, 0:3], in_=bk[:, 509:512], func=AF.Identity),
                         reads=self.bb(bi), writes=[bhcur[c]])
                    for s in (1, 2, 3):
                        P.op("dve", lambda e, bk=bk, xcc=xcc, c=c, s=s: e.scalar_tensor_tensor(
                            out=xcc[:, s:512], in0=bk[:, 0:512 - s], scalar=pv(c, 3 - s), in1=xcc[:, s:512], op0=ALU.mult, op1=ALU.add),
                            reads=self.bb(bi) + [bx, self.b_const], writes=[bx])
                        if q > 0:
                            P.op("dve", lambda e, xcc=xcc, c=c, s=s, hprev=hprev: e.scalar_tensor_tensor(
                                out=xcc[:, 0:s], in0=hprev[:, c, 3 - s:3], scalar=pv(c, 3 - s), in1=xcc[:, 0:s], op0=ALU.mult, op1=ALU.add),
                                reads=[bhprev[c], bx, self.b_const], writes=[bx])
                    P.op("pool", lambda e, xcc=xcc, xs=xs, kc=kc: e.tensor_copy(out=xcb[xs][:, kc, :], in_=xcc),
                         reads=[bx], writes=[b_xcb[xs][kc]])
                for phase in range(3):
                    for jc in range(2):
                        oc = 2 * hd + jc
                        ts_ = jc
                        ta, tx, at, mt, hs = tmp[ts_]
                        bta, btx, bat, bmt, bhs = b_tmp[ts_]
                        if phase == 0:
                            banks = []
                            for ax in range(2):
                                bi = self.next_bank()
                                bk = self.bank(bi)
                                banks.append((bi, bk))
                                for kc in range(2):
                                    gi = (ax * 4 + hd) * 2 + kc
                                    P.op("pe", lambda e, bk=bk, gi=gi, jc=jc, xs=xs, kc=kc: e.matmul(
                                        bk, lhsT=gw[:, gi, jc * 128:(jc + 1) * 128], rhs=xcb[xs][:, kc, :], start=(kc == 0), stop=(kc == 1)),
                                        reads=[b_gw, b_xcb[xs][kc]], writes=self.bb(bi))
                            (bia, bka), (bix, bkx) = banks
                            self.act(ta, bka, AF.Tanh, self.bb(bia) + [b_der], [bta], scale=0.5, bias=dv(6, oc))
                            self.act(tx, bkx, AF.Tanh, self.bb(bix) + [b_der], [btx], scale=0.5, bias=dv(7, oc))
                            self.act(at, ta, AF.Exp, [bta, b_der], [bat], scale=dv(4, oc), bias=dv(4, oc))
                            self.act(mt, ta, AF.Exp, [bta, b_der], [bmt], scale=dv(5, oc), bias=dv(5, oc))
                            self.act(mt, mt, AF.Relu, [bmt, self.b_const], [bmt], scale=-1.0, bias=self.one_ap)
                        elif phase == 1:
                            self.act(mt, mt, AF.Sqrt, [bmt], [bmt])
                        else:
                            xcc = xc[xs][:, jc, :]
                            self.stt(tx, tx, 1.0, xcc, ALU.add, ALU.mult, [btx, b_xc[xs][jc]], [btx])
                            self.stt(tx, tx, 0.5, mt, ALU.mult, ALU.mult, [btx, bmt], [btx])
                            init = 0.0 if q == 0 else cprev[:, oc:oc + 1]
                            P.op("dve", lambda e, hs=hs, at=at, tx=tx, init=init: e.tensor_tensor_scan(
                                out=hs, data0=at, data1=tx, initial=init, op0=ALU.mult, op1=ALU.add),
                                reads=[bat, btx] + ([bcprev[oc]] if q > 0 else []), writes=[bhs])
                            self.act(ccur[:, oc:oc + 1], hs[:, 511:512], AF.Identity, [bhs], [bccur[oc]])
                            self.tt("pool", obf[:, oc, :], hs, gate[:, oc, :], ALU.mult, [bhs, b_gate[oc]], [b_obf[oc]])
            for dch in range(NCH):
                bi = self.next_bank()
                bk = self.bank(bi)
                for k in range(NCH):
                    P.op("pe", lambda e, bk=bk, dch=dch, k=k: e.matmul(bk, lhsT=wout[:, k, dch * 128:(dch + 1) * 128], rhs=obf[:, k, :],
                                                                     start=(k == 0), stop=(k == NCH - 1)),
                         reads=[b_wout[k], b_obf[k]], writes=self.bb(bi))
                P.op("dve", lambda e, bk=bk, dch=dch, q=q: e.tensor_tensor(
                    out=xT[:, dch, q * 512:(q + 1) * 512], in0=bk, in1=xT[:, dch, q * 512:(q + 1) * 512], op=ALU.add),
                    reads=self.bb(bi) + self.bx(dch, q * 512, 512), writes=self.bx(dch, q * 512, 512))
        A.release(m0)


    def act(self, out, in_, func, r, w, **kw):
        return self.P.op("act", lambda e: e.activation(out=out, in_=in_, func=func, **kw), reads=r, writes=w)

    def mm(self, out, lhsT, rhs, r, w, start=True, stop=True, **kw):
        return self.P.op("pe", lambda e: e.matmul(out, lhsT=lhsT, rhs=rhs, start=start, stop=stop, **kw), reads=r, writes=w)

    def tr(self, out, in_, ident, r, w):
        return self.P.op("pe", lambda e: e.transpose(out=out, in_=in_, identity=ident), reads=r, writes=w)

    def tt(self, eng, out, in0, in1, op, r, w):
        return self.P.op(eng, lambda e: e.tensor_tensor(out=out, in0=in0, in1=in1, op=op), reads=r, writes=w)

    def ts(self, eng, out, in0, s1, s2, op0, op1, r, w):
        if s2 is None:
            return self.P.op(eng, lambda e: e.tensor_scalar(out=out, in0=in0, scalar1=s1, scalar2=None, op0=op0), reads=r, writes=w)
        return self.P.op(eng, lambda e: e.tensor_scalar(out=out, in0=in0, scalar1=s1, scalar2=s2, op0=op0, op1=op1), reads=r, writes=w)

    def stt(self, out, in0, scalar, in1, op0, op1, r, w):
        return self.P.op("dve", lambda e: e.scalar_tensor_tensor(out=out, in0=in0, scalar=scalar, in1=in1, op0=op0, op1=op1),
                         reads=r, writes=w)

    def cp(self, eng, out, in_, r, w):
        return self.P.op(eng, lambda e: e.tensor_copy(out=out, in_=in_), reads=r, writes=w)

    def recip(self, out, in_, r, w):
        return self.P.op("dve", lambda e: e.reciprocal(out=out, in_=in_), reads=r, writes=w)

    def qtile(self, i):
        return self.psum[:, i * 128:(i + 1) * 128]

    def declare_even_inputs(self):
        d = self.din
        self.abfm_d = d("ab_fm", [2, 20, 128, NCH * 128])
        self.abtm_d = d("ab_tm", [2, 128, NCH * 1544])
        self.abwout_d = d("ab_wout", [2, NCH, 128, D])
        self.epv_d = d("e_pv", [128, 2 * 50])
        self.ebc_d = d("e_bc", [1040])

    def load_even_consts(self):
        P = self.P
        self.epv = self.es.enter_context(self.nc.sbuf_tensor("epv_s", [128, 100], F32))
        self.ebc = self.es.enter_context(self.nc.sbuf_tensor("ebc_s", [128, 16], F32))
        P.op("sp", lambda e: e.dma_start(out=self.epv[:], in_=self.epv_d), writes=[self.b_const], dma="c0")
        P.op("sp", lambda e: e.dma_start(out=self.ebc[:], in_=self.ebc_d[0:16].partition_broadcast(128)),
             writes=[self.b_const], dma="c0")

    def even_mixer(self, l):
        P, A, xT = self.P, self.arena, self.xT
        j = l // 2
        NB = self.NB
        TQ = NB * 128
        NQ = T // TQ
        m0 = A.mark()
        C = self.cst
        U128, SL128, MSL, U16, O16, IND8 = C(C_U128), C(C_SL128), C(C_MSL), C(C_U16), C(C_O16), C(C_IND8, 8)
        bc = self.b_const
        wfm = [A.alloc([128, NCH * 128], BF16) for _ in range(3)]
        b_wfm = [Buf("wfm") for _ in range(3)]
        wtm = A.alloc([128, NCH, 1544], BF16)
        b_wtm = [Buf("wtm") for _ in range(NCH)]
        wout = A.alloc([128, NCH, D], BF16)
        b_wout = [Buf("wout") for _ in range(NCH)]
        wtm_src = self.abtm_d[j].rearrange("p (k f) -> p k f", k=NCH)
        for k in range(NCH):
            P.op("pool", lambda e, k=k: e.dma_start(out=wtm[:, k, :], in_=wtm_src[:, k, :]), writes=[b_wtm[k]], dma="mw%d" % (k % 2))
        for k in range(NCH):
            P.op("pool", lambda e, k=k: e.dma_start(out=wout[:, k, :], in_=self.abwout_d[j, k]), writes=[b_wout[k]], dma="mw%d" % (k % 2))
        fm_list = [(qi, ci) for qi in range(NQ) for ci in range(20)]
        fm_state = {"n": 0}

        def load_fm():
            n = fm_state["n"]
            if n >= len(fm_list):
                return
            fm_state["n"] += 1
            ci = fm_list[n][1]
            s = n % 3
            P.op("pool", lambda e: e.dma_start(out=wfm[s], in_=self.abfm_d[j, ci]), writes=[b_wfm[s]], dma="wfm%d" % s)

        load_fm()
        load_fm()
        lay = A.alloc([128, 16], F32)
        b_lay = Buf("lay")
        self.act(lay[:, 0:4], self.ebc[:, j * 4:j * 4 + 4], AF.Exp, [bc], [b_lay])
        self.ts("dve", lay[:, 0:4], lay[:, 0:4], -1.0, None, ALU.mult, None, [b_lay], [b_lay])
        self.cp("dve", lay[:, 4:8], self.ebc[:, 8 + j * 4:8 + j * 4 + 4], [bc], [b_lay])
        nA, dtb = lay[:, 0:4], lay[:, 4:8]
        lb = A.alloc([128, 512], F32)
        oml = A.alloc([128, 512], F32)
        b_lb = Buf("lb")
        m1 = A.mark()
        raw = A.alloc([128, 2, 512], F32)
        tmpl = A.alloc([128, 4, 512], F32)
        b_raw = Buf("raw")
        P.op("sp", lambda e: e.dma_start(out=raw.rearrange("p a b -> p (a b)"), in_=self.ebc_d[16:1040].partition_broadcast(128)),
             writes=[b_raw], dma="c0")
        R_, W_ = [b_raw, b_lb], [b_raw, b_lb]
        mx, e0, e1, s_ = tmpl[:, 0, :], tmpl[:, 1, :], tmpl[:, 2, :], tmpl[:, 3, :]
        self.tt("dve", mx, raw[:, 0, :], raw[:, 1, :], ALU.max, R_, W_)
        self.tt("dve", e0, raw[:, 0, :], mx, ALU.subtract, R_, W_)
        self.tt("dve", e1, raw[:, 1, :], mx, ALU.subtract, R_, W_)
        self.act(e0, e0, AF.Exp, R_, W_)
        self.act(e1, e1, AF.Exp, R_, W_)
        self.tt("dve", s_, e0, e1, ALU.add, R_, W_)
        self.recip(s_, s_, R_, W_)
        self.tt("dve", e0, e0, s_, ALU.mult, R_, W_)
        self.tt("dve", e1, e1, s_, ALU.mult, R_, W_)
        if j == 0:
            self.tt("dve", lb, e0, e0, ALU.subtract, R_, W_)
        else:
            self.tt("dve", mx, e0, e1, ALU.add, R_, W_)
            self.tt("dve", lb, mx, e0, ALU.subtract, R_, W_)
        self.ts("dve", oml, lb, -1.0, 1.0, ALU.mult, ALU.add, R_, W_)
        A.release(m1)
        P.fence()
        pvq = lambda ci, k: self.epv[:, j * 50 + ci * 4 + k:j * 50 + ci * 4 + k + 1]
        gnorm = self.epv[:, j * 50 + 48:j * 50 + 49]
        hnorm = self.epv[:, j * 50 + 49:j * 50 + 50]

        H0 = 4
        hTq = A.alloc([128, NCH, TQ + H0], BF16)
        b_hq = [Buf("hq") for _ in range(NCH)]
        b_hh = Buf("hh")
        oTq = A.alloc([128, NCH, TQ], BF16)
        b_oq = [Buf("oq") for _ in range(NCH)]
        rs = A.alloc([128, TQ], F32)
        b_rs = Buf("rs")
        kT = A.alloc([128, 4, TQ], F32)
        kTb = A.alloc([128, 4, TQ], BF16)
        qT = A.alloc([128, 4, TQ], BF16)
        vT = A.alloc([128, 4, TQ], F32)
        zg = A.alloc([128, 4, TQ], BF16)
        gbg = A.alloc([128, 4, TQ], BF16)
        b_kT = [Buf("kT") for _ in range(4)]
        b_kTb = [Buf("kTb") for _ in range(4)]
        b_qT = [Buf("qT") for _ in range(4)]
        b_vT = [Buf("vT") for _ in range(4)]
        b_zg = [Buf("zg") for _ in range(4)]
        b_gbg = [Buf("gbg") for _ in range(4)]
        ctmp = [A.alloc([128, TQ], F32) for _ in range(2)]
        b_ct = [Buf("ct") for _ in range(2)]
        qtTf = A.alloc([128, 512], F32)
        b_qtTf = Buf("qtTf")
        qsl = A.alloc([128, NB, 512], F32)
        fraw = A.alloc([128, NB, 512], F32)
        vbf = A.alloc([128, NB, 512], BF16)
        baraw = A.alloc([128, NB, 8], F32)
        b_qsl = [Buf("qsl") for _ in range(NB)]
        b_fraw = [Buf("fraw") for _ in range(NB)]
        b_vbf = [Buf("vbf") for _ in range(NB)]
        b_ba = [Buf("ba") for _ in range(NB)]
        sqn = A.alloc([128, TQ], BF16)
        rn = ctmp[0]
        b_sqn, b_rn = Buf("sqn"), b_ct[0]
        NFT = 8
        gf = [[A.alloc([128, 128], F32) for _ in range(NFT)] for _ in range(4)]
        b_gf = [[Buf("gf") for _ in range(NFT)] for _ in range(4)]
        NBT = 9
        gb = [[A.alloc([128, 128], BF16) for _ in range(NBT)] for _ in range(4)]
        b_gb = [[Buf("gb") for _ in range(NBT)] for _ in range(4)]
        Sg = A.alloc([128, 4, 128], F32)
        Sgb = A.alloc([128, 4, 128], BF16)
        b_Sg = [Buf("Sg") for _ in range(4)]
        b_Sgb = [Buf("Sgb") for _ in range(4)]
        sc = [A.alloc([128, 48], F32) for _ in range(2)]
        b_sc = [Buf("sc") for _ in range(2)]
        Fh = A.alloc([128, 512], F32)
        KKh = A.alloc([128, 512], F32)
        EXh = A.alloc([128, 512], F32)
        kdh = A.alloc([128, 512], BF16)
        vmh = [A.alloc([128, 512], BF16) for _ in range(2)]
        qtTh = A.alloc([128, 512], BF16)
        ktTh = A.alloc([128, 512], BF16)
        scTh = A.alloc([128, 512], BF16)
        Sh = A.alloc([128, 512], F32)
        dSh = A.alloc([128, 32], F32)
        b_F, b_KK, b_EX, b_kd = [Buf(n) for n in ("F", "KK", "EX", "kd")]
        b_vm = [Buf("vm") for _ in range(2)]
        b_qtT, b_ktT, b_dS = Buf("qtT"), Buf("ktT"), Buf("dS")
        b_scT = [Buf("scT") for _ in range(4)]
        b_Sh = [Buf("Sh") for _ in range(4)]
        for h in range(4):
            P.op("dve", lambda e, h=h: e.memset(Sg[:, h, :], 0.0), writes=[b_Sg[h]])
            P.op("dve", lambda e, h=h: e.memset(Sgb[:, h, :], 0.0), writes=[b_Sgb[h]])
            P.op("dve", lambda e, h=h: e.memset(Sh[:, h * 128:(h + 1) * 128], 0.0), writes=[b_Sh[h]])
        bk = self.b_bk
        prot = {"i": 0}
        PB = [7, 0, 1, 2, 3, 4]

        def next_half():
            b_ = PB[prot["i"] % len(PB)]
            prot["i"] += 1
            return self.psum[:, b_ * 512: (b_ + 1) * 512], [bk[b_]]

        hb5, hb6 = self.bank(5), self.bank(6)
        B5, B6 = [bk[5]] * 4, [bk[6]] * 4
        tmsel = {"i": 0}

        for qi in range(NQ):
            tok0 = qi * TQ
            tsl = slice(tok0, tok0 + TQ)
            xb = lambda c: self.bx(c, tok0, TQ)
            if qi == 0:
                P.op("pool", lambda e: e.memset(hTq[:, :, 0:H0], 0.0), writes=[b_hh])
            else:
                self.cp("pool", hTq[:, :, 0:H0], hTq[:, :, TQ:TQ + H0], list(b_hq) + [b_hh], [b_hh])
            for c in range(NCH):
                self.act(oTq[:, c, :], xT[:, c, tsl], AF.Square, xb(c), [b_oq[c]])
            psb, pb = next_half()
            ps = psb[:, 0:TQ]
            for c in range(NCH):
                self.mm(ps, self.ones_bf, oTq[:, c, :], [b_oq[c], bc], pb, start=(c == 0), stop=(c == NCH - 1))
            self.act(rs, ps, AF.Ln, pb + [bc], [b_rs], scale=1.0 / D, bias=self.eps_ap)
            self.act(rs, rs, AF.Exp, [b_rs], [b_rs], scale=-0.5)
            for c in range(NCH):
                g = self.gains[:, l * NCH + c: l * NCH + c + 1]
                self.stt(hTq[:, c, H0:H0 + TQ], xT[:, c, tsl], g, rs, ALU.mult, ALU.mult, xb(c) + [b_rs, bc, b_hh], [b_hq[c]])
            for ci in range(20):
                n = qi * 20 + ci
                s = n % 3
                load_fm()
                psb, pb = next_half()
                h = ci % 4
                if ci < 12:
                    ps = psb[:, 0:TQ + H0]
                    for k in range(NCH):
                        self.mm(ps, wfm[s][:, k * 128:(k + 1) * 128], hTq[:, k, 0:H0 + TQ], [b_wfm[s], b_hq[k], b_hh], pb,
                                start=(k == 0), stop=(k == NCH - 1))
                    ct, bct = ctmp[ci % 2], b_ct[ci % 2]
                    tap = lambda sft: ps[:, H0 - sft:H0 - sft + TQ]
                    self.act(ct, tap(0), AF.Identity, pb + [bc], [bct], scale=pvq(ci, 3))
                    for sft in (1, 2, 3):
                        self.stt(ct, tap(sft), pvq(ci, 3 - sft), ct, ALU.mult, ALU.add, pb + [bct, bc], [bct])
                    if ci < 4:
                        self.act(qT[:, h, :], ct, AF.Silu, [bct], [b_qT[h]])
                    elif ci < 8:
                        self.act(kT[:, h, :], ct, AF.Silu, [bct], [b_kT[h]])
                    else:
                        self.act(vT[:, h, :], ct, AF.Silu, [bct], [b_vT[h]])
                else:
                    ps = psb[:, 0:TQ]
                    for k in range(NCH):
                        self.mm(ps, wfm[s][:, k * 128:(k + 1) * 128], hTq[:, k, H0:H0 + TQ], [b_wfm[s], b_hq[k]], pb,
                                start=(k == 0), stop=(k == NCH - 1))
                    if ci < 16:
                        self.act(zg[:, h, :], ps, AF.Silu, pb, [b_zg[h]])
                    else:
                        self.act(gbg[:, h, :], ps, AF.Silu, pb, [b_gbg[h]])
            for b in range(NB):
                lo = b * 128
                for seg in range(3):
                    i = tmsel["i"]
                    tmsel["i"] = 1 - i
                    pst, pbt = (hb5, B5) if i == 0 else (hb6, B6)
                    for k in range(NCH):
                        self.mm(pst, hTq[:, k, H0 + lo:H0 + lo + 128], wtm[:, k, seg * 512:(seg + 1) * 512], [b_hq[k], b_wtm[k]], pbt,
                                start=(k == 0), stop=(k == NCH - 1))
                    if seg == 0:
                        self.act(qsl[:, b, :], pst, AF.Silu, pbt, [b_qsl[b]])
                    elif seg == 1:
                        self.act(fraw[:, b, :], pst, AF.Identity, pbt, [b_fraw[b]])
                    else:
                        self.act(vbf[:, b, :], pst, AF.Identity, pbt, [b_vbf[b]])
                i = tmsel["i"]
                tmsel["i"] = 1 - i
                pst, pbt = (hb5, B5) if i == 0 else (hb6, B6)
                for k in range(NCH):
                    self.mm(pst[:, 0:8], hTq[:, k, H0 + lo:H0 + lo + 128], wtm[:, k, 1536:1544], [b_hq[k], b_wtm[k]], pbt,
                            start=(k == 0), stop=(k == NCH - 1))
                self.act(baraw[:, b, :], pst[:, 0:8], AF.Identity, pbt, [b_ba[b]])
            for half_ in range(2):
                combos = [(h, isk) for h in range(2 * half_, 2 * half_ + 2) for isk in range(2)]
                for i, (h, isk) in enumerate(combos):
                    src, bsrc = (kT, b_kT) if isk else (qT, b_qT)
                    self.act(oTq[:, i, :], src[:, h, :], AF.Square, [bsrc[h]], [b_oq[i]])
                pss = []
                for i, (h, isk) in enumerate(combos):
                    psb, pb = next_half()
                    ps = psb[:, 0:TQ]
                    self.mm(ps, self.ones_bf, oTq[:, i, :], [b_oq[i], bc], pb)
                    pss.append((ps, pb))
                for i, (h, isk) in enumerate(combos):
                    ps, pb = pss[i]
                    self.act(ps, ps, AF.Ln, pb + [bc], pb, bias=self.eps_ap)
                    self.act(ps, ps, AF.Exp, pb, pb, scale=-0.5)
                for i, (h, isk) in enumerate(combos):
                    ps, pb = pss[i]
                    if isk:
                        self.tt("dve", kT[:, h, :], kT[:, h, :], ps, ALU.mult, [b_kT[h]] + pb, [b_kT[h]])
                        self.cp("pool", kTb[:, h, :], kT[:, h, :], [b_kT[h]], [b_kTb[h]])
                    else:
                        self.stt(qT[:, h, :], qT[:, h, :], float(128 ** -0.5), ps, ALU.mult, ALU.mult, [b_qT[h]] + pb, [b_qT[h]])
            def chain_gen(b):
                scb, bscb = sc[b % 2], b_sc[b % 2]
                beta, g_tm, gc_sb, eg, dch, ekd, beg = (scb[:, 0:4], scb[:, 4:8], scb[:, 8:12], scb[:, 12:16], scb[:, 16:20],
                                                       scb[:, 20:24], scb[:, 24:28])
                t_a, t_b = scb[:, 28:32], scb[:, 32:36]
                RW = ([bscb], [bscb])
                self.act(t_a, baraw[:, b, 0:4], AF.Exp, [b_ba[b], bscb], [bscb], scale=-1.0)
                self.ts("dve", t_a, t_a, 1.0, None, ALU.add, None, *RW)
                self.recip(beta, t_a, *RW)
                self.tt("dve", t_b, baraw[:, b, 4:8], dtb, ALU.add, [b_ba[b], b_lay, bscb], [bscb])
                yield
                self.act(t_b, t_b, AF.Exp, *RW)
                self.act(t_b, t_b, AF.Ln, [bscb, bc], [bscb], bias=self.one_ap)
                self.tt("dve", g_tm, t_b, nA, ALU.mult, [bscb, b_lay], [bscb])
                yield
                t0 = self.bank(0)
                self.mm(t0[:, 0:4], U128, g_tm, [bc, bscb], [bk[0]])
                self.mm(t0[:, 4:8], self.ones_f, g_tm, [bc, bscb], [bk[0]])
                self.act(gc_sb, t0[:, 0:4], AF.Identity, [bk[0], bscb], [bscb])
                self.act(eg, t0[:, 0:4], AF.Exp, [bk[0], bscb], [bscb])
                self.act(dch, t0[:, 4:8], AF.Exp, [bk[0], bscb], [bscb])
                self.tt("dve", ekd, t0[:, 4:8], gc_sb, ALU.subtract, [bk[0], bscb], [bscb])
                yield
                self.act(ekd, ekd, AF.Exp, *RW)
                self.tt("dve", beg, beta, eg, ALU.mult, *RW)

            for _ in chain_gen(0):
                pass
            for b in range(NB):
                lo = b * 128
                bsl = slice(lo, lo + 128)
                scb, bscb = sc[b % 2], b_sc[b % 2]
                beta, g_tm, gc_sb, eg, dch, ekd, beg = (scb[:, 0:4], scb[:, 4:8], scb[:, 8:12], scb[:, 12:16], scb[:, 16:20],
                                                       scb[:, 20:24], scb[:, 24:28])
                Lc = dict(locals())
                gens = [self.gdn_chunk(Lc), self.hgrn_block(Lc)]
                if b + 1 < NB:
                    gens.append(chain_gen(b + 1))
                while gens:
                    for gen in list(gens):
                        try:
                            next(gen)
                        except StopIteration:
                            gens.remove(gen)
            for dch_ in range(NCH):
                psb, pb = next_half()
                ps = psb[:, 0:TQ]
                for k in range(NCH):
                    self.mm(ps, wout[:, k, dch_ * 128:(dch_ + 1) * 128], oTq[:, k, :], [b_wout[k], b_oq[k]], pb,
                            start=(k == 0), stop=(k == NCH - 1))
                self.tt("dve", xT[:, dch_, tsl], ps, xT[:, dch_, tsl], ALU.add, pb + self.bx(dch_, tok0, TQ), self.bx(dch_, tok0, TQ))
        A.release(m0)

    def gdn_chunk(self, L):
        bk, bc = L["bk"], L["bc"]
        U128, SL128, MSL = L["U128"], L["SL128"], L["MSL"]
        kT, kTb, qT, vT, zg = L["kT"], L["kTb"], L["qT"], L["vT"], L["zg"]
        b_kT, b_kTb, b_qT, b_vT, b_zg = L["b_kT"], L["b_kTb"], L["b_qT"], L["b_vT"], L["b_zg"]
        gf, b_gf, gb, b_gb = L["gf"], L["b_gf"], L["gb"], L["b_gb"]
        Sg, Sgb, b_Sg, b_Sgb = L["Sg"], L["Sgb"], L["b_Sg"], L["b_Sgb"]
        bscb, bsl = L["bscb"], L["bsl"]
        beta, g_tm, eg, dch, ekd, beg = L["beta"], L["g_tm"], L["eg"], L["dch"], L["ekd"], L["beg"]
        oTq, b_oq, gnorm = L["oTq"], L["b_oq"], L["gnorm"]
        idf = self.ident_f
        HS = range(4)
        T_ = lambda h, s: self.psum[:, (1 + h) * 512 + s * 128: (1 + h) * 512 + (s + 1) * 128]
        B_ = lambda h: [bk[1 + h]]
        T0 = lambda h: self.psum[:, h * 128:(h + 1) * 128]
        B0 = [bk[0]]
        col = lambda a, h: a[:, h:h + 1]
        for h in HS:
            self.tr(T_(h, 0), kT[:, h, bsl], idf, [b_kT[h], bc], B_(h))
            self.tr(T_(h, 1), vT[:, h, bsl], idf, [b_vT[h], bc], B_(h))
            self.ts("dve", gf[h][7], U128, col(g_tm, h), None, ALU.mult, None, [bc, bscb], [b_gf[h][7]])
        for h in HS:
            self.act(gb[h][1], T_(h, 1), AF.Identity, B_(h) + [bscb], [b_gb[h][1]], scale=col(beta, h))
            self.act(gb[h][2], T_(h, 0), AF.Identity, B_(h) + [bscb], [b_gb[h][2]], scale=col(beg, h))
            self.ts("dve", gb[h][3], T_(h, 0), col(ekd, h), None, ALU.mult, None, B_(h) + [bscb], [b_gb[h][3]])
        yield
        for h in HS:
            kb = kT[:, h, bsl]
            self.mm(T_(h, 2), kb, kb, [b_kT[h]], B_(h))
            self.mm(T_(h, 3), gf[h][7], SL128, [b_gf[h][7], bc], B_(h))
            self.mm(T_(h, 0), SL128, gf[h][7], [b_gf[h][7], bc], B_(h))
            self.mm(T_(h, 1), self.ones_f, gf[h][7], [b_gf[h][7], bc], B_(h))
        for h in HS:
            self.act(gf[h][5], T_(h, 3), AF.Exp, B_(h), [b_gf[h][5]])
            self.act(gf[h][6], T_(h, 0), AF.Exp, B_(h), [b_gf[h][6]])
            self.act(gb[h][0], T_(h, 1), AF.Exp, B_(h), [b_gb[h][0]])
            self.tt("dve", gf[h][0], T_(h, 2), gf[h][5], ALU.mult, B_(h) + [b_gf[h][5]], [b_gf[h][0]])
            self.stt(gf[h][0], gf[h][0], col(beta, h), MSL, ALU.mult, ALU.mult, [b_gf[h][0], bscb, bc], [b_gf[h][0]])
            self.tt("pool", gb[h][4], qT[:, h, bsl], gb[h][0], ALU.mult, [b_qT[h], b_gb[h][0]], [b_gb[h][4]])
        yield
        for h in HS:
            self.tr(T_(h, 2), gf[h][0], idf, [b_gf[h][0], bc], B_(h))
            self.mm(T_(h, 3), kTb[:, h, bsl], qT[:, h, bsl], [b_kTb[h], b_qT[h]], B_(h))
        for h in HS:
            self.act(gf[h][2], T_(h, 2), AF.Identity, B_(h), [b_gf[h][2]])
            self.tt("dve", gf[h][4], idf, T_(h, 2), ALU.subtract, B_(h) + [bc], [b_gf[h][4]])
            self.tt("dve", gf[h][6], T_(h, 3), gf[h][6], ALU.mult, B_(h) + [b_gf[h][6]], [b_gf[h][6]])
            self.tt("pool", gb[h][5], gf[h][6], U128, ALU.mult, [b_gf[h][6], bc], [b_gb[h][5]])
        yield
        pi, pti = 0, 2
        for m in range(1, 7):
            npi, npti = 1 - pi, 5 - pti
            for h in HS:
                self.mm(T_(h, 0), gf[h][pti], gf[h][pi], [b_gf[h][pti], b_gf[h][pi]], B_(h))
                if m < 6:
                    self.mm(T_(h, 1), gf[h][pi], gf[h][pti], [b_gf[h][pti], b_gf[h][pi]], B_(h))
            for h in HS:
                self.act(gf[h][npi], T_(h, 0), AF.Identity, B_(h), [b_gf[h][npi]])
                if m < 6:
                    self.cp("dve", gf[h][npti], T_(h, 1), B_(h), [b_gf[h][npti]])
            yield
            for h in HS:
                self.mm(T_(h, 2), gf[h][npi], gf[h][4], [b_gf[h][npi], b_gf[h][4]], B_(h))
            for h in HS:
                self.tt("dve", gf[h][4], gf[h][4], T_(h, 2), ALU.add, B_(h) + [b_gf[h][4]], [b_gf[h][4]])
            pi, pti = npi, npti
            yield
        for h in HS:
            self.act(gb[h][8], gf[h][4], AF.Identity, [b_gf[h][4]], [b_gb[h][8]])
        for h in HS:
            self.mm(T_(h, 2), gb[h][8], gb[h][1], [b_gb[h][8], b_gb[h][1]], B_(h))
            self.mm(T_(h, 3), gb[h][2], gb[h][8], [b_gb[h][8], b_gb[h][2]], B_(h))
        for h in HS:
            self.act(gf[h][7], T_(h, 2), AF.Identity, B_(h), [b_gf[h][7]])
            self.cp("dve", gb[h][6], T_(h, 3), B_(h), [b_gb[h][6]])
        yield
        for h in HS:
            self.mm(T_(h, 0), gb[h][6], Sgb[:, h, :], [b_gb[h][6], b_Sgb[h]], B_(h))
        for h in HS:
            self.tt("dve", gb[h][7], gf[h][7], T_(h, 0), ALU.subtract, B_(h) + [b_gf[h][7]], [b_gb[h][7]])
        yield
        for h in HS:
            self.mm(T_(h, 1), Sgb[:, h, :], gb[h][4], [b_Sgb[h], b_gb[h][4]], B_(h), start=True, stop=False, skip_group_check=True)
            self.mm(T_(h, 1), gb[h][7], gb[h][5], [b_gb[h][7], b_gb[h][5]], B_(h), start=False, stop=True, skip_group_check=True)
            self.mm(T_(h, 3), gb[h][3], gb[h][7], [b_gb[h][3], b_gb[h][7]], B_(h))
        for h in HS:
            self.stt(Sg[:, h, :], Sg[:, h, :], col(dch, h), T_(h, 3), ALU.mult, ALU.add, B_(h) + [b_Sg[h], bscb], [b_Sg[h]])
            self.act(Sgb[:, h, :], Sg[:, h, :], AF.Identity, [b_Sg[h]], [b_Sgb[h]])
        yield
        for h in HS:
            self.act(gb[h][0], T_(h, 1), AF.Square, B_(h), [b_gb[h][0]])
        for h in HS:
            self.mm(T_(h, 2), self.ones_bf, gb[h][0], [bc, b_gb[h][0]], B_(h))
        yield
        for h in HS:
            self.act(gf[h][5], T_(h, 2), AF.Ln, B_(h) + [bc], [b_gf[h][5]], scale=1.0 / 128, bias=self.eps_ap)
            self.act(gf[h][5], gf[h][5], AF.Exp, [b_gf[h][5]], [b_gf[h][5]], scale=-0.5)
            self.stt(gf[h][6], T_(h, 1), gnorm, gf[h][5], ALU.mult, ALU.mult, B_(h) + [b_gf[h][5], bc], [b_gf[h][6]])
            self.tt("pool", oTq[:, h, bsl], gf[h][6], zg[:, h, bsl], ALU.mult, [b_gf[h][6], b_zg[h]], [b_oq[h]])

    def hgrn_block(self, L):
        P = self.P
        bc = L["bc"]
        U16, O16, IND8 = L["U16"], L["O16"], L["IND8"]
        b, bsl = L["b"], L["bsl"]
        qsl, fraw, vbf, b_qsl, b_fraw, b_vbf = L["qsl"], L["fraw"], L["vbf"], L["b_qsl"], L["b_fraw"], L["b_vbf"]
        lb, oml, b_lb = L["lb"], L["oml"], L["b_lb"]
        Fh, KKh, EXh, kdh = L["Fh"], L["KKh"], L["EXh"], L["kdh"]
        b_F, b_KK, b_EX, b_kd = L["b_F"], L["b_KK"], L["b_EX"], L["b_kd"]
        vmh, b_vm, qtTh, ktTh, scTh = L["vmh"], L["b_vm"], L["qtTh"], L["ktTh"], L["scTh"]
        b_qtT, b_ktT, b_scT, b_dS = L["b_qtT"], L["b_ktT"], L["b_scT"], L["b_dS"]
        Sh, dSh, b_Sh, qtTf, b_qtTf = L["Sh"], L["dSh"], L["b_Sh"], L["qtTf"], L["b_qtTf"]
        gbg, b_gbg, oTq, b_oq, hnorm = L["gbg"], L["b_gbg"], L["oTq"], L["b_oq"], L["hnorm"]
        gf, b_gf, gb, b_gb = L["gf"], L["b_gf"], L["gb"], L["b_gb"]
        hb5, hb6, B5, B6 = L["hb5"], L["hb6"], L["B5"], L["B6"]
        idf = self.ident_f
        hc = lambda a, h: a[:, h * 128:(h + 1) * 128]
        self.act(Fh, fraw[:, b, :], AF.Exp, [b_fraw[b]], [b_F], scale=-1.0)
        self.ts("dve", Fh, Fh, 1.0, None, ALU.add, None, [b_F], [b_F])
        self.recip(Fh, Fh, [b_F], [b_F])
        self.tt("dve", Fh, Fh, oml, ALU.mult, [b_F, b_lb], [b_F])
        self.tt("dve", Fh, Fh, lb, ALU.add, [b_F, b_lb], [b_F])
        self.ts("pool", KKh, Fh, -1.0, 1.0, ALU.mult, ALU.add, [b_F], [b_KK])
        self.ts("dve", Fh, Fh, 1e-30, None, ALU.max, None, [b_F, b_KK], [b_F])
        self.act(Fh, Fh, AF.Ln, [b_F], [b_F])
        yield
        self.mm(hb5, U16, Fh, [bc, b_F], B5)
        self.mm(hb6, O16, Fh, [bc, b_F], B6)
        qth, b_qt = qsl[:, b, :], b_qsl[b]
        kth, b_kt = KKh, b_KK
        self.act(EXh, hb6, AF.Exp, B6, [b_EX])
        self.tt("pool", kdh, KKh, EXh, ALU.mult, [b_KK, b_EX], [b_kd])
        self.act(EXh, hb5, AF.Exp, B5 + [b_EX], [b_EX])
        self.tt("dve", qth, qth, EXh, ALU.mult, [b_qt, b_EX], [b_qt])
        self.act(EXh, hb5, AF.Exp, B5 + [b_EX], [b_EX], scale=-1.0)
        self.tt("dve", kth, KKh, EXh, ALU.mult, [b_KK, b_EX, b_kd], [b_kt])
        yield
        for h in range(4):
            self.tr(hc(hb5, h), hc(qth, h), idf, [b_qt, bc], [B5[h]])
            self.tr(hc(hb6, h), hc(kth, h), idf, [b_kt, bc], [B6[h]])
        self.act(qtTh, hb5, AF.Identity, B5, [b_qtT])
        self.cp("dve", qtTf, hb5, B5, [b_qtTf])
        self.cp("dve", ktTh, hb6, B6, [b_ktT])
        yield
        for h in range(4):
            self.mm(hc(hb5, h), hc(ktTh, h), hc(qtTh, h), [b_ktT, b_qtT], [B5[h]])
        for h in range(4):
            self.tt("dve", hc(scTh, h), hc(hb5, h), U16, ALU.mult, [B5[h], bc], [b_scT[h]])
        yield
        for h in range(4):
            self.mm(hb6[:, h * 8:(h + 1) * 8], hc(Fh, h), IND8, [b_F, bc], [B6[0]])
        self.act(dSh, hb6[:, 0:32], AF.Exp, [B6[0]], [b_dS])
        yield
        for h in range(4):
            self.mm(hc(hb5, h), hc(vbf[:, b, :], h), hc(scTh, h), [b_vbf[b], b_scT[h]], [B5[h]],
                    start=(h == 0), stop=False, skip_group_check=True)
        for c in range(8):
            vm, bvm = vmh[c % 2], b_vm[c % 2]
            self.ts("pool", vm, vbf[:, b, :], IND8[:, c:c + 1], 1.0, ALU.mult, ALU.mult, [b_vbf[b], bc], [bvm])
            for h in range(4):
                self.mm(hc(hb5, h)[:, c * 16:(c + 1) * 16], hc(Sh, h), hc(qtTf, h)[:, c * 16:(c + 1) * 16],
                        [b_Sh[h], b_qtTf], [B5[h]], start=False, stop=(c == 7), skip_group_check=True)
            for h in range(4):
                self.mm(hc(hb6, h), hc(kdh, h), hc(vm, h), [b_kd, bvm], [B6[h]])
            for h in range(4):
                self.stt(hc(Sh, h), hc(Sh, h), dSh[:, h * 8 + c:h * 8 + c + 1], hc(hb6, h), ALU.mult, ALU.add,
                         [B6[h], b_Sh[h], b_dS], [b_Sh[h]])
            yield
        yield
        for h in range(4):
            self.act(hc(kdh, h), hc(hb5, h), AF.Square, [B5[h], b_kd], [b_kd])
        for h in range(4):
            self.mm(hc(hb6, h), self.ones_bf, hc(kdh, h), [bc, b_kd], [B6[h]])
        yield
        for h in range(4):
            self.act(hc(EXh, h), hc(hb6, h), AF.Ln, [B6[h], bc, b_EX], [b_EX], scale=1.0 / 128, bias=self.eps_ap)
            self.act(hc(EXh, h), hc(EXh, h), AF.Exp, [b_EX], [b_EX], scale=-0.5)
            self.stt(hc(Fh, h), hc(hb5, h), hnorm, hc(EXh, h), ALU.mult, ALU.mult, [B5[h], b_EX, b_F, bc], [b_F])
            self.tt("pool", oTq[:, 4 + h, bsl], hc(Fh, h), gbg[:, h, bsl], ALU.mult, [b_F, b_gbg[h]], [b_oq[4 + h]])


FULL_PLAN = [("even", 0), ("ffn", 0), ("odd", 1), ("ffn", 1), ("even", 2), ("ffn", 2), ("odd", 3), ("ffn", 3)]
_CACHE = {}


def prep_shared(inp):
    f = lambda k: np.asarray(inp[k], np.float32)
    sh = {}
    sh["consts"] = make_consts()
    g = np.concatenate([f("norm_mix"), f("norm_ffn"), f("norm_final")[None, :]], axis=0)
    sh["gains"] = pvec(g).reshape(128, 9 * NCH)
    cw = np.concatenate([f("ffn_conv_w"), f("ffn_conv_b")[:, None, :]], axis=1)
    sh["ffn_cw"] = np.ascontiguousarray(pvec(cw).transpose(0, 1, 3, 2)).reshape(128, DEPTH * NFC * 4)
    wup = f("ffn_w_up")
    sh["ffn_wup"] = np.stack([chunk_cols(wup[l], 2 * NFC) for l in range(DEPTH)]).reshape(DEPTH, 2 * NFC, 128, NCH * 128)
    sh["ffn_wdn"] = np.ascontiguousarray(f("ffn_w_down").reshape(DEPTH, NFC, 128, D))
    prep_mixers(inp, sh)
    return sh


def prep_mixers(inp, sh):
    f = lambda k: np.asarray(inp[k], np.float32)
    cwin = f("c_w_in")
    sh["c_win"] = np.stack([chunk_cols(cwin[j], 16) for j in range(2)]).reshape(2, 16, 128, NCH * 128)
    sh["c_gw"] = np.ascontiguousarray(np.stack([f("c_gate_a_w"), f("c_gate_x_w")], axis=1))
    sh["c_wout"] = np.ascontiguousarray(f("c_w_out").reshape(2, NCH, 128, D))
    pvs = np.concatenate([f("c_conv_w"), f("c_conv_b")[:, None], f("c_gate_a_b")[:, None], f("c_gate_x_b")[:, None],
                          f("c_lambda")[:, None]], axis=1)
    sh["c_pv"] = np.ascontiguousarray(pvec(pvs).transpose(0, 1, 3, 2)).reshape(128, 2 * NCH * 8)
    prep_even(inp, sh)


def prep_even(inp, sh):
    f = lambda k: np.asarray(inp[k], np.float32)
    w = f("ab_w_in")
    fm = np.concatenate([w[:, :, 0:2048], w[:, :, 3592:4104]], axis=2)
    sh["ab_fm"] = np.stack([chunk_cols(fm[j], 20) for j in range(2)]).reshape(2, 20, 128, NCH * 128)
    tm = np.concatenate([w[:, :, 2056:3592], w[:, :, 2048:2056]], axis=2)
    sh["ab_tm"] = np.ascontiguousarray(tm.reshape(2, NCH, 128, 1544).transpose(0, 2, 1, 3)).reshape(2, 128, NCH * 1544)
    sh["ab_wout"] = np.ascontiguousarray(f("ab_w_out").reshape(2, NCH, 128, D))
    cw = np.ascontiguousarray(pvec(f("gdn_conv_w")).transpose(0, 1, 3, 2)).reshape(128, 2, 48)
    gn = f("gdn_norm").T[:, :, None]
    hn = f("hgrn_norm").T[:, :, None]
    sh["e_pv"] = np.ascontiguousarray(np.concatenate([cw, gn, hn], axis=2)).reshape(128, 100)
    sh["e_bc"] = np.ascontiguousarray(np.concatenate([f("gdn_a_log").reshape(-1), f("gdn_dt_bias").reshape(-1),
                                                      f("hgrn_lower_bounds").reshape(-1)]))


def run(inp, plan, n_cores=8, splits=None):
    sh = prep_shared(inp)
    x = np.asarray(inp["x"], np.float32)
    xts = [np.ascontiguousarray(x[b].T).reshape(NCH, 128, T) for b in range(n_cores)]
    parts = splits if splits is not None else [plan]
    for pi, part in enumerate(parts):
        last = pi == len(parts) - 1
        key = (tuple(part), last)
        if key not in _CACHE:
            _CACHE[key] = Builder(list(part), do_final_norm=last).build()
        nc = _CACHE[key]
        in_maps = []
        for b in range(n_cores):
            m = dict(sh)
            m["xT"] = xts[b]
            in_maps.append(m)
        res = run_bass_kernel_spmd(nc, in_maps, core_ids=list(range(n_cores)))
        xts = [np.ascontiguousarray(r["yT"]) for r in res.results]
    out = np.stack([np.ascontiguousarray(r.reshape(D, T).T) for r in xts])
    return out


N_SPLIT = 1


def kernel(**inputs):
    if N_SPLIT == 1:
        return run(inputs, FULL_PLAN, 8).astype(np.float32)
    return run(inputs, FULL_PLAN, 8, splits=[FULL_PLAN[:4], FULL_PLAN[4:]]).astype(np.float32)
```
